# Optimizing a Trainium2 kernel written in Bass

```python
import math
import jax, jax.numpy as jnp
from jax import lax
import numpy as np

D_MODEL = 2048
BATCH = 4
SEQ = 2048
DEPTH = 2

GRID_W = 64
CTX_LEN = 256
D_MIX = D_MODEL
FOURIER_W = D_MODEL // 4
FOURIER_GROUPS = 4
FOURIER_GW = FOURIER_W // FOURIER_GROUPS
CONV_W = D_MODEL // 4
CONV_K = 31
ATT_W = D_MODEL // 2
DIFF_HEAD_DIM = 64
DIFF_HEADS = ATT_W // (2 * DIFF_HEAD_DIM)
D_FF = 4 * D_MODEL
Q_BLOCK = 128
ROPE_BASE = 10000.0
EPS = 1e-6

OFF_CONV = FOURIER_W
OFF_Q = OFF_CONV + 2 * CONV_W
OFF_K = OFF_Q + ATT_W
OFF_V = OFF_K + ATT_W
IN_COLS = OFF_V + ATT_W

kernel_name = "hybrid_fourier_conformer_diffattn_dit_block"


def rms_norm(x, g):
    xf = x.astype(jnp.float32)
    y = xf * lax.rsqrt(jnp.mean(xf * xf, axis=-1, keepdims=True) + EPS)
    return (y * g.astype(jnp.float32)).astype(x.dtype)


def layer_norm(x, g, b):
    xf = x.astype(jnp.float32)
    mu = jnp.mean(xf, axis=-1, keepdims=True)
    xc = xf - mu
    y = xc * lax.rsqrt(jnp.mean(xc * xc, axis=-1, keepdims=True) + EPS)
    return (y * g.astype(jnp.float32) + b.astype(jnp.float32)).astype(x.dtype)


def modulate(h, shift, scale):
    return h * (1 + scale) + shift


def axial_rope(n):
    rows = n // GRID_W
    row = jnp.repeat(jnp.arange(rows), GRID_W).astype(jnp.float32)
    col = jnp.tile(jnp.arange(GRID_W), rows).astype(jnp.float32)
    n_freq = DIFF_HEAD_DIM // 4
    inv = ROPE_BASE ** (-jnp.arange(n_freq, dtype=jnp.float32) / n_freq)
    ang = jnp.concatenate([row[:, None] * inv, col[:, None] * inv], axis=-1)
    return jnp.cos(ang), jnp.sin(ang)


def apply_rope(t, cos, sin):
    half = DIFF_HEAD_DIM // 2
    tf = t.astype(jnp.float32)
    cs = cos[None, :, None, None, :]
    sn = sin[None, :, None, None, :]
    t1, t2 = tf[..., :half], tf[..., half:]
    out = jnp.concatenate([t1 * cs - t2 * sn, t2 * cs + t1 * sn], axis=-1)
    return out.astype(t.dtype)


def fourier_mixer(u, w_f):
    b, L, _ = u.shape
    ug = u.reshape(b, L, FOURIER_GROUPS, FOURIER_GW).astype(jnp.float32)
    z = jnp.fft.fft2(ug, axes=(1, 3), norm="ortho").real.astype(u.dtype)
    return jnp.einsum('blgc,gce->blge', z, w_f).reshape(b, L, FOURIER_W)


def conv_module(u, w_dw, b_dw, g_ln, b_ln, w_pw, b_pw):
    a, gt = jnp.split(u, 2, axis=-1)
    z = a * jax.nn.sigmoid(gt)
    z = lax.conv_general_dilated(
        z, w_dw[:, None, :], window_strides=(1,),
        padding=[(CONV_K // 2, CONV_K // 2)],
        dimension_numbers=('NWC', 'WIO', 'NWC'),
        feature_group_count=CONV_W) + b_dw
    z = jax.nn.silu(layer_norm(z, g_ln, b_ln))
    return z @ w_pw + b_pw


def diff_attend(q, k, v, lam):
    s = jnp.einsum('bqhmd,bkhmd->bhmqk', q, k).astype(jnp.float32) * (DIFF_HEAD_DIM ** -0.5)
    p = jax.nn.softmax(s, axis=-1)
    a = p[:, :, 0] - lam * p[:, :, 1]
    return jnp.einsum('bhqk,bkhe->bqhe', a.astype(v.dtype), v)


def split_heads_qk(t):
    b, L, _ = t.shape
    return t.reshape(b, L, DIFF_HEADS, 2, DIFF_HEAD_DIM)


def split_heads_v(t):
    b, L, _ = t.shape
    return t.reshape(b, L, DIFF_HEADS, 2 * DIFF_HEAD_DIM)


def setup_inputs(seed: int = 0) -> dict:
    key = jax.random.key(seed)
    ks = jax.random.split(key, 32)
    f32 = jnp.float32
    nrm = lambda k, shape, s: jax.random.normal(k, shape, f32) * s
    gain = lambda k, shape: 1.0 + 0.05 * jax.random.normal(k, shape, f32)
    L = DEPTH
    return {
        "x": nrm(ks[0], (BATCH, SEQ, D_MODEL), 1.0),
        "c": nrm(ks[1], (BATCH, D_MODEL), 1.0),
        "ctx": nrm(ks[2], (BATCH, CTX_LEN, D_MODEL), 1.0),
        "c_ctx": nrm(ks[3], (D_MODEL,), 1.0),
        "w_ada": nrm(ks[4], (L, D_MODEL, 6 * D_MODEL), 0.5 * D_MODEL ** -0.5),
        "b_ada": nrm(ks[5], (L, 6 * D_MODEL), 0.02),
        "g_pre_mix": gain(ks[6], (L, D_MODEL)),
        "g_post_mix": gain(ks[7], (L, D_MODEL)),
        "g_pre_mlp": gain(ks[8], (L, D_MODEL)),
        "g_post_mlp": gain(ks[9], (L, D_MODEL)),
        "w_in": nrm(ks[10], (L, D_MODEL, IN_COLS), D_MODEL ** -0.5),
        "w_out": nrm(ks[11], (L, D_MIX, D_MODEL), D_MIX ** -0.5),
        "w_fourier": nrm(ks[12], (L, FOURIER_GROUPS, FOURIER_GW, FOURIER_GW), FOURIER_GW ** -0.5),
        "w_dw": nrm(ks[13], (L, CONV_K, CONV_W), CONV_K ** -0.5),
        "b_dw": nrm(ks[14], (L, CONV_W), 0.02),
        "g_conv_ln": gain(ks[15], (L, CONV_W)),
        "b_conv_ln": nrm(ks[16], (L, CONV_W), 0.02),
        "w_conv_pw": nrm(ks[17], (L, CONV_W, CONV_W), CONV_W ** -0.5),
        "b_conv_pw": nrm(ks[18], (L, CONV_W), 0.02),
        "lambda_q1": nrm(ks[19], (L, DIFF_HEAD_DIM), 0.1),
        "lambda_k1": nrm(ks[20], (L, DIFF_HEAD_DIM), 0.1),
        "lambda_q2": nrm(ks[21], (L, DIFF_HEAD_DIM), 0.1),
        "lambda_k2": nrm(ks[22], (L, DIFF_HEAD_DIM), 0.1),
        "g_subln": gain(ks[23], (L, 2 * DIFF_HEAD_DIM)),
        "w_mlp_in": nrm(ks[24], (L, D_MODEL, D_FF), D_MODEL ** -0.5),
        "w_mlp_out": nrm(ks[25], (L, D_FF, D_MODEL), D_FF ** -0.5),
    }


def reference(x, c, ctx, c_ctx, w_ada, b_ada, g_pre_mix, g_post_mix, g_pre_mlp, g_post_mlp,
              w_in, w_out, w_fourier, w_dw, b_dw, g_conv_ln, b_conv_ln, w_conv_pw, b_conv_pw,
              lambda_q1, lambda_k1, lambda_q2, lambda_k2, g_subln, w_mlp_in, w_mlp_out):
    bsz, n, _ = x.shape
    n_blocks = n // Q_BLOCK
    cos, sin = axial_rope(n)
    cx = ctx
    for l in range(DEPTH):
        last = l == DEPTH - 1
        mod_lat = (jax.nn.silu(c) @ w_ada[l] + b_ada[l])[:, None, :]
        mod_ctx = jax.nn.silu(c_ctx) @ w_ada[l] + b_ada[l]
        sh1, sc1, g1, sh2, sc2, g2 = jnp.split(mod_lat, 6, axis=-1)
        csh1, csc1, cg1, csh2, csc2, cg2 = jnp.split(mod_ctx, 6, axis=-1)

        lam_init = 0.8 - 0.6 * math.exp(-0.3 * l)
        lq1 = lambda_q1[l].astype(jnp.float32); lk1 = lambda_k1[l].astype(jnp.float32)
        lq2 = lambda_q2[l].astype(jnp.float32); lk2 = lambda_k2[l].astype(jnp.float32)
        lam = jnp.exp(jnp.sum(lq1 * lk1)) - jnp.exp(jnp.sum(lq2 * lk2)) + lam_init

        def attn_post(o):
            o = rms_norm(o, g_subln[l]) * (1 - lam_init)
            return o.reshape(o.shape[0], o.shape[1], ATT_W)

        def mixer_concat(f_in, cv_in, attn_out):
            yf = fourier_mixer(f_in, w_fourier[l])
            yc = conv_module(cv_in, w_dw[l], b_dw[l], g_conv_ln[l], b_conv_ln[l],
                             w_conv_pw[l], b_conv_pw[l])
            return jnp.concatenate([yf, yc, attn_out], axis=-1) @ w_out[l]

        h = modulate(rms_norm(x, g_pre_mix[l]), sh1, sc1)
        hc = modulate(rms_norm(cx, g_pre_mix[l]), csh1, csc1)
        p = h @ w_in[l]
        f_in, cv_in, q, k, v = jnp.split(p, [OFF_CONV, OFF_Q, OFF_K, OFF_V], axis=-1)
        pc_kv = hc @ w_in[l][:, OFF_K:]
        kc, vc = jnp.split(pc_kv, 2, axis=-1)

        q_h = apply_rope(split_heads_qk(q), cos, sin)
        k_h = apply_rope(split_heads_qk(k), cos, sin)
        kc_h = split_heads_qk(kc)
        k_all = jnp.concatenate([kc_h, k_h], axis=1)
        v_all = jnp.concatenate([split_heads_v(vc), split_heads_v(v)], axis=1)

        qb = jnp.moveaxis(q_h.reshape(bsz, n_blocks, Q_BLOCK, DIFF_HEADS, 2, DIFF_HEAD_DIM), 1, 0)
        ob = lax.map(lambda qq: diff_attend(qq, k_all, v_all, lam), qb)
        o = jnp.moveaxis(ob, 0, 1).reshape(bsz, n, DIFF_HEADS, 2 * DIFF_HEAD_DIM)
        y = mixer_concat(f_in, cv_in, attn_post(o))

        if not last:
            pc_rest = hc @ w_in[l][:, :OFF_K]
            fc_in, cvc_in, qc = jnp.split(pc_rest, [OFF_CONV, OFF_Q], axis=-1)
            oc = diff_attend(split_heads_qk(qc), kc_h, split_heads_v(vc), lam)
            yc = mixer_concat(fc_in, cvc_in, attn_post(oc))
            cx = cx + cg1 * rms_norm(yc, g_post_mix[l])
            hc2 = modulate(rms_norm(cx, g_pre_mlp[l]), csh2, csc2)
            yc2 = jnp.square(jax.nn.relu(hc2 @ w_mlp_in[l])) @ w_mlp_out[l]
            cx = cx + cg2 * rms_norm(yc2, g_post_mlp[l])

        x = x + g1 * rms_norm(y, g_post_mix[l])

        h2 = modulate(rms_norm(x, g_pre_mlp[l]), sh2, sc2)
        y2 = jnp.square(jax.nn.relu(h2 @ w_mlp_in[l])) @ w_mlp_out[l]
        x = x + g2 * rms_norm(y2, g_post_mlp[l])
    return x
```

```python
import os, math, contextlib
import numpy as np
import ml_dtypes
import concourse.bass as bass
import concourse.mybir as mybir
from concourse.bass_utils import run_bass_kernel_spmd

F32 = mybir.dt.float32
BF16 = mybir.dt.bfloat16
AF = mybir.ActivationFunctionType
ALU = mybir.AluOpType
AX = mybir.AxisListType

D = 2048
NL = 1024
NCX = 256
NT = NL + NCX
KC = 16
DEPTH = 2
EPS = 1e-6
IN_COLS = 4608
PAIRS = [[0, 1], [2, 3], [4, 5], [6, 7]]
TCS = [(0, 512), (512, 512), (1024, 256)]


class Tracker:
    ENGS = ("pe", "act", "dve", "pool", "sp")
    NDMA = {"sp": 8, "pool": 8, "act": 4}

    def __init__(self):
        self.ops = {e: [] for e in self.ENGS}
        self.count = {e: 0 for e in self.ENGS}
        self.last_write = {}
        self.readers = {}
        self.known = {e: {} for e in self.ENGS}
        self.dma_n = {q: 0 for q in self.NDMA}
        self.dma_latest = {}
        self.extra_sems = []

    def _add(self, waits, tok):
        if tok is None:
            return
        k, v = tok
        if waits.get(k, 0) < v:
            waits[k] = v

    def _deps(self, reads, writes):
        waits = {}
        for k in reads:
            self._add(waits, self.last_write.get(k))
        for k in writes:
            self._add(waits, self.last_write.get(k))
            for t in self.readers.get(k, ()):
                self._add(waits, t)
        return waits

    def _filter(self, eng, waits):
        out = []
        kn = self.known[eng]
        for k, v in waits.items():
            if k == eng and eng == "pe":
                continue
            if kn.get(k, 0) >= v:
                continue
            kn[k] = v
            out.append((k, v))
        return out

    def _record(self, tok, reads, writes):
        for k in reads:
            self.readers.setdefault(k, []).append(tok)
        for k in writes:
            self.last_write[k] = tok
            self.readers[k] = []

    def op(self, eng, reads, writes, fn):
        waits = self._filter(eng, self._deps(reads, writes))
        self.count[eng] += 1
        tok = (eng, self.count[eng])
        self.ops[eng].append((waits, fn, eng, 1))
        self._record(tok, reads, writes)
        return tok

    def dma(self, q, reads, writes, fn):
        n = self.dma_n[q]
        ns = self.NDMA[q]
        key = ("dma", q, n % ns)
        waits = self._deps(reads, writes)
        prev = 16 * (n // ns)
        if prev > 0:
            self._add(waits, (key, prev))
        waits = self._filter(q, waits)
        self.dma_n[q] = n + 1
        tok = (key, prev + 16)
        self.dma_latest[key] = prev + 16
        self.ops[q].append((waits, fn, key, 16))
        self._record(tok, reads, writes)
        return tok

    def custom(self, eng, reads, writes, fn, key, amt):
        waits = self._filter(eng, self._deps(reads, writes))
        if key not in self.extra_sems:
            self.extra_sems.append(key)
        v = self.dma_latest.get(key, 0) + amt
        self.dma_latest[key] = v
        tok = (key, v)
        self.ops[eng].append((waits, fn, key, amt))
        self._record(tok, reads, writes)
        return tok

    def barrier(self):
        toks = [(e, self.count[e]) for e in self.ENGS if self.count[e] > 0]
        toks += list(self.dma_latest.items())
        for e in self.ENGS:
            waits = {}
            for t in toks:
                self._add(waits, t)
            waits = self._filter(e, waits)
            if waits:
                self.ops[e].append((waits, None, None, 0))
        self.last_write = {}
        self.readers = {}

    def final_wait(self, eng, toks):
        waits = {}
        for t in toks:
            self._add(waits, t)
        waits = self._filter(eng, waits)
        self.ops[eng].append((waits, None, None, 0))

    def emit(self, nc, stack):
        sems = {}
        for e in self.ENGS:
            sems[e] = stack.enter_context(nc.semaphore("s_" + e))
        for q, ns in self.NDMA.items():
            for j in range(ns):
                sems[("dma", q, j)] = stack.enter_context(nc.semaphore("d_%s%d" % (q, j)))
        for i, k in enumerate(self.extra_sems):
            sems[k] = stack.enter_context(nc.semaphore("x%d" % i))
        block = stack.enter_context(nc.Block())

        def run(eng_name):
            def body(eng):
                for waits, fn, inc_key, amt in self.ops[eng_name]:
                    for k, v in waits:
                        eng.wait_ge(sems[k], v)
                    if fn is not None:
                        ins = fn(eng)
                        ins.then_inc(sems[inc_key], amt)
            return body

        block.tensor(run("pe"))
        block.scalar(run("act"))
        block.vector(run("dve"))
        block.gpsimd(run("pool"))
        block.sync(run("sp"))


def vec_layout():
    lay = {}
    off = 0

    def add(name, w):
        nonlocal off
        lay[name] = (off, w)
        off += w

    for l in range(DEPTH):
        add("badap_%d" % l, 60)
        for g in ("gpm", "gqm", "gpf", "gqf"):
            add("%s2_%d" % (g, l), 32)
        add("wdw_%d" % l, 124)
        for g in ("bdw", "gln", "bln", "bpw"):
            add("%s_%d" % (g, l), 4)
        add("gsub_%d" % l, 1)
        for g in ("lq1", "lk1", "lq2", "lk2"):
            add("%s_%d" % (g, l), 64)
    add("cvec5", 80)
    add("bsel", 4)
    add("mask", 2)
    return lay, off


VLAY, NV = vec_layout()


def chunked(v):
    v = np.asarray(v, np.float32)
    return np.ascontiguousarray(v.reshape(-1, 128).T)


def dup2(a):
    return np.ascontiguousarray(np.repeat(a, 2, axis=1))


def build(stage=99, dbg=False, wseq=None):
    nc = bass.Bass("TRN2", target_bir_lowering=False)
    T = Tracker()

    def din(name, shape, dt=F32):
        return nc.dram_tensor(name, list(shape), dt, kind="ExternalInput").ap()

    xT_d = din("xT", [D, NL])
    cxT_d = din("cxT", [D, NCX])
    vec_d = din("vec", [128, NV])
    w_ada_d = din("w_ada", [DEPTH, D, 1536])
    w_in_d = din("w_in", [DEPTH, D, IN_COLS])
    w_out_d = din("w_out", [DEPTH, D, D])
    w_f_d = din("w_f", [DEPTH, 128, 4, 128])
    w_pw_d = din("w_pw", [DEPTH, 512, 512])
    w_m1_d = din("w_m1", [DEPTH, D, 4 * D])
    w_m2_d = din("w_m2", [DEPTH, 4 * D, D])
    dft_d = din("dftc", [128, 512], BF16)
    CL_d = din("CL", [2048, NL], BF16)
    SL_d = din("SLn", [2048, NL], BF16)
    Cc_d = din("Cc", [NCX, NCX], BF16)
    Sc_d = din("Scn", [NCX, NCX], BF16)
    rope_d = din("rope", [128, 2, NL])
    perm_d = din("perm", [128, 128])
    ident_d = din("ident", [128, 128])
    out_d = nc.dram_tensor("outT", [D, NL], F32, kind="ExternalOutput").ap()

    xs_d = nc.dram_tensor("xs", [KC, 128, NT], F32)
    kin = nc.dram_tensor("kin", [1024, NL], BF16)
    kout = nc.dram_tensor("kout", [2048, NL], BF16)
    vin = nc.dram_tensor("vin", [NL, 1024], BF16)
    vout = nc.dram_tensor("vout", [2 * NL, 1024], BF16)
    fin = nc.dram_tensor("fin", [NL, 1024], BF16)
    fout = nc.dram_tensor("fout", [2 * NL, 1024], BF16)
    zin = nc.dram_tensor("zin", [512, 30], BF16)
    zout = nc.dram_tensor("zout", [1024, 30], BF16)
    gin = nc.dram_tensor("gin", [128, 120], F32)
    gout = nc.dram_tensor("gout", [1024, 120], F32)

    dbg_outs = []

    BASE, LIMIT = 16512, 229344

    class Arena:
        def __init__(self, start, limit):
            self.off, self.limit = start, limit

        def take(self, name, shape, dt):
            nb = int(np.prod(shape[1:])) * (4 if dt == F32 else 2)
            nb = (nb + 63) // 64 * 64
            assert self.off + nb <= self.limit, (name, self.off, nb, self.limit)
            t = nc.alloc_sbuf_tensor_at(name, list(shape), dt, offset=self.off)
            self.off += nb
            return t

    P = Arena(BASE, LIMIT)
    YB = P.take("YB", [128, KC, NT], F32)
    yb_off = BASE
    HM = P.take("HM", [128, KC, NT], BF16)
    WB = [P.take("WB%d" % i, [128, 8192], BF16) for i in range(2)]
    VEC = P.take("VEC", [128, NV], F32)
    MOD = P.take("MOD", [128, 2, 96, 2], F32)
    DER = P.take("DER", [128, 2, 4, 32], F32)
    LAM = P.take("LAM", [128, 16], F32)
    LT = P.take("LT", [128, 64], F32)
    ones16 = P.take("ones16", [128, 128], BF16)
    perm16 = P.take("perm16", [128, 128], BF16)
    ident16 = P.take("ident16", [128, 128], BF16)
    sc16 = P.take("sc16", [128, 80], BF16)
    GP = P.take("GP", [128, 2, 12, 5], F32)
    rs = P.take("rs", [128, NT], F32)
    sq = P.take("sq", [128, 2, NT], BF16)
    tn = P.take("tn", [128, 2, NT], F32)
    f_start = P.off
    Fm = Arena(f_start, LIMIT)
    hid = Fm.take("hid", [128, 8, NT], BF16)
    rl = Fm.take("rl", [128, 2, 512], F32)
    R1 = Arena(yb_off, yb_off + KC * NT * 4)
    QT = R1.take("QT", [128, 8, NT], BF16)
    zT = R1.take("zT", [128, 4, NL + 30], BF16)
    zcT = R1.take("zcT", [128, 4, NCX + 30], BF16)
    KcT = R1.take("KcT", [128, 8, NCX], BF16)
    Vc = R1.take("Vc", [128, 2, 1024], BF16)
    uabc = R1.take("uabc", [128, 2, 1024], BF16)
    r2_start = R1.off
    YB_END = yb_off + KC * NT * 4

    class Arena2:
        def __init__(self):
            self.segs = [[r2_start, YB_END], [f_start, LIMIT]]

        def take(self, name, shape, dt):
            nb = int(np.prod(shape[1:])) * (4 if dt == F32 else 2)
            nb = (nb + 63) // 64 * 64
            for s in self.segs:
                if s[0] + nb <= s[1]:
                    t = nc.alloc_sbuf_tensor_at(name, list(shape), dt, offset=s[0])
                    s[0] += nb
                    return t
            raise AssertionError(("R2 overflow", name, nb, self.segs))

    A2 = Arena2()
    sg = A2.take("sg", [128, 2, NT], F32)
    finT = A2.take("finT", [128, 4, NT], BF16)
    rope = A2.take("rope", [128, 2, NL], F32)
    t32 = A2.take("t32", [128, 2, 512], F32)
    tb = A2.take("tb", [128, 2, 512], BF16)
    vv = A2.take("vv", [128, 2, 512], F32)
    kst = A2.take("kst", [128, 2, 512], BF16)
    vst = A2.take("vst", [128, 2, 512], BF16)
    uast = A2.take("uast", [128, 2, 1024], BF16)
    hl = A2.take("hl", [128, 4, 2, 15], BF16)
    wAB = A2.take("wAB", [128, 4, 256], BF16)
    wf32 = A2.take("wf32", [128, 4, 128], F32)
    dft32 = A2.take("dft32", [128, 512], BF16)
    wfh = A2.take("wfh", [128, 4, 128], BF16)
    wfl = A2.take("wfl", [128, 4, 128], BF16)
    A3 = Arena2()
    czf = A3.take("czf", [128, 4, NT], F32)
    sz = A3.take("sz", [128, 4, NT], BF16)
    wpw = A3.take("wpw", [128, 4, 512], BF16)
    Dg = A3.take("Dg", [128, 2, 31, 128], BF16)
    mean = tn[:, 0, :]
    ex2 = tn[:, 1, :]
    crs = rs
    A4 = Arena2()
    Kh = A4.take("Kh", [128, 2, 2048], BF16)
    Vh = A4.take("Vh", [128, 2, 16, 128], BF16)
    pt = A4.take("pt", [128, 4, 512], BF16)
    pts = A4.take("pts", [128, 2, 512], BF16)
    frA = A4.take("frA", [128, 2, 512], F32)
    fo2 = A4.take("fo2", [128, 2, 512], F32)
    fot = A4.take("fot", [128, 2, 256], F32)
    sqo = A4.take("sqo", [128, 2, 256], BF16)
    frB = A4.take("frB", [128, 2, 256], F32)
    uabj = A4.take("uabj", [128, 3, 1024], BF16)
    tab = A4.take("tab", [128, 3, 2, 1024], BF16)
    tabc = A4.take("tabc", [128, 2, 2, 256], BF16)

    PB = [nc.alloc_psum_tensor("pb%d" % i, [128, 1024], F32) for i in range(4)]

    def bank(b):
        return PB[b // 2][:, (b % 2) * 512:(b % 2) * 512 + 512]

    def V(name, lo=0, hi=None):
        o, w = VLAY[name]
        hi = w if hi is None else hi
        return VEC[:, o + lo:o + hi]

    def dump(name, ap_src, shape, dt, reads):
        if not dbg:
            return
        t = nc.dram_tensor("dbg_" + name, list(shape), dt, kind="ExternalOutput")
        dbg_outs.append(T.dma("sp", reads, [("dbg", name)], lambda e: e.dma_start(out=t.ap(), in_=ap_src)))

    wb_n = [0]
    wb_issued = [0]
    wrec = []
    WD = {"w_ada": w_ada_d, "w_in": w_in_d, "w_out": w_out_d, "w_m1": w_m1_d, "w_m2": w_m2_d}

    def _wview(k, desc):
        _, _, kc0, kc1, c0, c1 = desc
        a, b = kc1 - kc0, c1 - c0
        return WB[k % 2][:, 0:a * b].rearrange("p (a b) -> p a b", a=a)

    def _issue_w(k):
        desc = wseq[k]
        wname, l_, kc0, kc1, c0, c1 = desc
        src_ap = WD[wname][l_].rearrange("(kc p) n -> p kc n", p=128)[:, kc0:kc1, c0:c1]
        view = _wview(k, desc)
        T.dma("pool", [], [("WB", k % 2)], lambda e: e.dma_start(out=view, in_=src_ap))

    def load_w(*desc):
        k = wb_n[0]
        wb_n[0] += 1
        view = _wview(k, desc)
        if wseq is None:
            wrec.append(tuple(desc))
            return view, ("WB", k % 2)
        assert tuple(wseq[k]) == tuple(desc), (k, wseq[k], desc)
        while wb_issued[0] <= min(k + 1, len(wseq) - 1):
            _issue_w(wb_issued[0])
            wb_issued[0] += 1
        return view, ("WB", k % 2)

    def mm_group(out_ap, pairs, reads, writes):
        def fn(e):
            n = len(pairs)
            ins = None
            for i, (l, r) in enumerate(pairs):
                ins = e.matmul(out_ap, lhsT=l, rhs=r, start=(i == 0), stop=(i == n - 1))
            return ins
        return T.op("pe", reads, writes, fn)

    def act(out, in_, func, reads, writes, bias=0.0, scale=1.0):
        return T.op("act", reads, writes,
                    lambda e: e.activation(out=out, in_=in_, func=func, bias=bias, scale=scale))

    def tt(eng, out, in0, in1, op, reads, writes):
        return T.op(eng, reads, writes, lambda e: e.tensor_tensor(out=out, in0=in0, in1=in1, op=op))

    def stt(out, in0, scalar, in1, op0, op1, reads, writes):
        return T.op("dve", reads, writes, lambda e: e.scalar_tensor_tensor(
            out=out, in0=in0, scalar=scalar, in1=in1, op0=op0, op1=op1))

    def ts(out, in0, s1, s2, op0, op1, reads, writes, eng="dve"):
        if s2 is None:
            return T.op(eng, reads, writes, lambda e: e.tensor_scalar(
                out=out, in0=in0, scalar1=s1, scalar2=None, op0=op0))
        return T.op(eng, reads, writes, lambda e: e.tensor_scalar(
            out=out, in0=in0, scalar1=s1, scalar2=s2, op0=op0, op1=op1))

    def hmk(c, ti):
        return ("HM", c, ti)

    def ybk(c, ti):
        return ("YB", c, ti)

    T.dma("sp", [], ["VEC"], lambda e: e.dma_start(out=VEC[:], in_=vec_d[:, :]))
    T.dma("pool", [], ["perm16"], lambda e: e.dma_start(out=perm16[:], in_=perm_d[:, :]))
    T.dma("pool", [], ["ident16"], lambda e: e.dma_start(out=ident16[:], in_=ident_d[:, :]))
    T.op("dve", [], ["ones16"], lambda e: e.memset(ones16[:], 1.0))
    for c in range(KC):
        T.dma("sp", [], [ybk(c, 0), ybk(c, 1)], lambda e, c=c: e.dma_start(
            out=YB[:, c, 0:NL], in_=xT_d[c * 128:(c + 1) * 128, :]))
        T.dma("sp", [], [ybk(c, 2)], lambda e, c=c: e.dma_start(
            out=YB[:, c, NL:NT], in_=cxT_d[c * 128:(c + 1) * 128, :]))
    act(sc16[:], V("cvec5"), AF.Silu, ["VEC"], ["sc16"])
    for l in range(DEPTH):
        for pc in range(3):
            slot, wkey = load_w("w_ada", l, 0, 16, pc * 512, (pc + 1) * 512)
            pbank = pc % 2
            psA = bank(pbank)

            def fn(e, slot=slot, psA=psA):
                ins = None
                for sub in range(4):
                    for kc in range(KC):
                        ins = e.matmul(psA[:, sub * 5:sub * 5 + 5], lhsT=slot[:, kc, sub * 128:(sub + 1) * 128],
                                       rhs=sc16[:, kc * 5:kc * 5 + 5], start=(kc == 0), stop=(kc == KC - 1))
                return ins
            T.op("pe", [wkey, "sc16"], [("ps", pbank)], fn)
            o, _ = VLAY["badap_%d" % l]
            tt("dve", GP[:, l, pc * 4:(pc + 1) * 4, :].rearrange("p a b -> p (a b)"), psA[:, 0:20],
               VEC[:, o + pc * 20:o + pc * 20 + 20], ALU.add, [("ps", pbank), "VEC"], ["GP"])
    T.dma("sp", ["GP"], ["gin"], lambda e: e.dma_start(out=gin[:, :], in_=GP[:].rearrange("p a b c -> p (a b c)")))
    T.custom("pool", ["gin"], ["gout"], lambda e: e.collective_compute(
        "AllGather", ALU.bypass, replica_groups=[list(range(8))], ins=[gin.ap().opt()], outs=[gout.ap().opt()]), ("ccg",), 1)
    Gt = tn[:, 0, 0:960]
    T.dma("sp", ["gout"], [("tn", 0)], lambda e: e.dma_start(
        out=Gt.rearrange("p (r n) -> p r n", r=8), in_=gout.ap().rearrange("(r p) n -> p r n", p=128)))
    Gv = Gt.rearrange("p (r l j n) -> p r l j n", r=8, l=2, j=12)
    so, _ = VLAY["bsel"]
    for l in range(DEPTH):
        m0 = MOD[:, l, :, 0].rearrange("p (r j) -> p r j", r=8)
        m1 = MOD[:, l, :, 1].rearrange("p (r j) -> p r j", r=8)
        ts(m0, Gv[:, :, l, :, 0], VEC[:, so:so + 1], None, ALU.mult, None, [("tn", 0), "VEC"], [("MOD", l)])
        for b_ in range(1, 4):
            stt(m0, Gv[:, :, l, :, b_], VEC[:, so + b_:so + b_ + 1], m0, ALU.mult, ALU.add, [("tn", 0), "VEC", ("MOD", l)], [("MOD", l)])
        T.op("dve", [("tn", 0)], [("MOD", l)], lambda e, m1=m1, l=l: e.tensor_copy(out=m1, in_=Gv[:, :, l, :, 4]))
    modkeys = [("MOD", l) for l in range(DEPTH)]
    lam_init = [0.8 - 0.6 * math.exp(-0.3 * l) for l in range(DEPTH)]
    for l in range(DEPTH):
        def modv(lo):
            return MOD[:, l, lo:lo + 16, :].rearrange("p a b -> p (a b)")
        stt(DER[:, l, 0, :], modv(16), 1.0, V("gpm2_%d" % l), ALU.add, ALU.mult, modkeys + ["VEC"], [("DER", l)])
        tt("dve", DER[:, l, 1, :], modv(32), V("gqm2_%d" % l), ALU.mult, modkeys + ["VEC"], [("DER", l)])
        stt(DER[:, l, 2, :], modv(64), 1.0, V("gpf2_%d" % l), ALU.add, ALU.mult, modkeys + ["VEC"], [("DER", l)])
        tt("dve", DER[:, l, 3, :], modv(80), V("gqf2_%d" % l), ALU.mult, modkeys + ["VEC"], [("DER", l)])
        tt("dve", LT[:], V("lq1_%d" % l), V("lk1_%d" % l), ALU.mult, ["VEC"], ["LT"])
        T.op("dve", ["LT"], [("LAM", l, 2)], lambda e, l=l: e.reduce_sum(out=LAM[:, l * 4 + 2:l * 4 + 3], in_=LT[:], axis=AX.X))
        tt("dve", LT[:], V("lq2_%d" % l), V("lk2_%d" % l), ALU.mult, ["VEC", ("LAM", l, 2)], ["LT"])
        T.op("dve", ["LT"], [("LAM", l, 3)], lambda e, l=l: e.reduce_sum(out=LAM[:, l * 4 + 3:l * 4 + 4], in_=LT[:], axis=AX.X))
        act(LAM[:, l * 4 + 2:l * 4 + 4], LAM[:, l * 4 + 2:l * 4 + 4], AF.Exp, [("LAM", l, 2), ("LAM", l, 3)], [("LAM", l, 2), ("LAM", l, 3)])
        tt("dve", LAM[:, l * 4:l * 4 + 1], LAM[:, l * 4 + 3:l * 4 + 4], LAM[:, l * 4 + 2:l * 4 + 3], ALU.subtract,
           [("LAM", l, 2), ("LAM", l, 3)], [("LAM", l, 0)])
        ts(LAM[:, l * 4:l * 4 + 1], LAM[:, l * 4:l * 4 + 1], -lam_init[l], None, ALU.add, None, [("LAM", l, 0)], [("LAM", l, 0)])
        ts(LAM[:, l * 4 + 1:l * 4 + 2], V("gsub_%d" % l), 1.0 - lam_init[l], None, ALU.mult, None, ["VEC"], [("LAM", l, 1)])
    dump("MOD", MOD[:].rearrange("p a b c -> p (a b c)"), [128, 384], F32, modkeys)
    dump("DER", DER[:].rearrange("p a b c -> p (a b c)"), [128, 256], F32, [("DER", 0), ("DER", 1)])
    dump("LAM", LAM[:], [128, 16], F32, [("LAM", l, i) for l in range(2) for i in range(4)])

    def scal(l, which, c, t):
        if which == "bA":
            return MOD[:, l, c, t:t + 1]
        if which == "bB":
            return MOD[:, l, 48 + c, t:t + 1]
        return DER[:, l, which, c * 2 + t:c * 2 + t + 1]

    def stats_chunk(c, src_ap, src_keys, tcis, nfeat_chunks, psb):
        b = c % 2
        lo, hi = TCS[tcis[0]][0], TCS[tcis[-1]][0] + TCS[tcis[-1]][1]
        act(sq[:, b, lo:hi], src_ap[:, lo:hi], AF.Square, src_keys, [("sq", b)])

        def fn(e, c=c, b=b):
            ins = None
            for ti in tcis:
                s, w = TCS[ti]
                ins = e.matmul(bank(psb + ti)[:, 0:w], lhsT=ones16[:], rhs=sq[:, b, s:s + w],
                               start=(c == 0), stop=(c == nfeat_chunks - 1))
            return ins
        T.op("pe", [("sq", b), "ones16"], [("ps", psb + ti) for ti in tcis], fn)

    def rstd_finish(tcis, out_rs, rs_key, div, psb):
        for ti in tcis:
            s, w = TCS[ti]
            act(out_rs[:, s:s + w], bank(psb + ti)[:, 0:w], AF.Ln, [("ps", psb + ti)], [(rs_key, ti)], bias=EPS, scale=1.0 / div)
            act(out_rs[:, s:s + w], out_rs[:, s:s + w], AF.Exp, [(rs_key, ti)], [(rs_key, ti)], scale=-0.5)

    def sumsq_rstd(src_fn, src_keys_fn, tcis, nfeat_chunks, out_rs, rs_key, div, psb):
        for c in range(nfeat_chunks):
            stats_chunk(c, src_fn(c), src_keys_fn(c), tcis, nfeat_chunks, psb)
        rstd_finish(tcis, out_rs, rs_key, div, psb)

    def norm_to_HM(l, which_scale, which_bias, tcis):
        lo, hi = TCS[tcis[0]][0], TCS[tcis[-1]][0] + TCS[tcis[-1]][1]
        for c in range(KC):
            b = c % 2
            tt("dve", tn[:, b, lo:hi], YB[:, c, lo:hi], rs[:, lo:hi], ALU.mult,
               [ybk(c, ti) for ti in tcis] + [("rs", ti) for ti in tcis], [("tn", b)])
            for ti in tcis:
                s, w = TCS[ti]
                t = 1 if ti == 2 else 0
                act(HM[:, c, s:s + w], tn[:, b, s:s + w], AF.Identity, [("tn", b), ("DER", l)] + modkeys, [hmk(c, ti)],
                    bias=scal(l, which_bias, c, t), scale=scal(l, which_scale, c, t))

    fin_toks = []
    for l in range(DEPTH):
        last = (l == DEPTH - 1)
        alltc = [0, 1, 2]
        mixtc = [0, 1] if last else [0, 1, 2]
        if l == 0:
            sumsq_rstd(lambda c: YB[:, c, :], lambda c: [ybk(c, 0), ybk(c, 1), ybk(c, 2)], alltc, KC, rs, "rs", D, 2)
        norm_to_HM(l, 0, "bA", alltc)
        if l == 0:
            dump("h0", HM[:].rearrange("p a b -> p (a b)"), [128, KC * NT], BF16, [hmk(c, ti) for c in range(KC) for ti in range(3)])
        if stage <= 1:
            break
        T.barrier()
        T.dma("sp", [], ["rope"], lambda e: e.dma_start(out=rope[:], in_=rope_d[:, :, :]))
        T.dma("sp", [], ["dft32"], lambda e: e.dma_start(out=dft32[:], in_=dft_d[:, :]))
        T.dma("sp", [], ["wf32"], lambda e, l=l: e.dma_start(out=wf32[:], in_=w_f_d[l]))
        for zz in (zT, zcT):
            T.op("pool", [], ["zhalo"], lambda e, zz=zz: e.memset(zz[:], 0.0))
        T.op("dve", ["wf32"], ["wfh"], lambda e: e.tensor_copy(out=wfh[:], in_=wf32[:]))
        tt("dve", wfl[:], wf32[:], wfh[:], ALU.subtract, ["wf32", "wfh"], ["wfl"])
        for g in range(4):
            def fn(e, g=g):
                for (bk, o_) in ((0, 0), (1, 128)):
                    ob_ = bank(bk)[:, g * 128:(g + 1) * 128]
                    e.matmul(ob_, lhsT=dft32[:, o_:o_ + 128], rhs=wfh[:, g, :], start=True, stop=False)
                    e.matmul(ob_, lhsT=dft32[:, o_:o_ + 128], rhs=wfl[:, g, :], start=False, stop=False)
                    ins = e.matmul(ob_, lhsT=dft32[:, 256 + o_:256 + o_ + 128], rhs=wfh[:, g, :], start=False, stop=True)
                return ins
            T.op("pe", ["dft32", "wfh", "wfl"], [("ps", 0), ("ps", 1)], fn)
        for g in range(4):
            act(wAB[:, g, 0:128], bank(0)[:, g * 128:(g + 1) * 128], AF.Copy, [("ps", 0)], ["wAB"])
            act(wAB[:, g, 128:256], bank(1)[:, g * 128:(g + 1) * 128], AF.Copy, [("ps", 1)], ["wAB"])
        wiv = w_in_d[l].rearrange("(kc p) n -> p kc n", p=128)
        psrr = [0]

        def next_ps():
            b = 2 + psrr[0] % 4
            psrr[0] += 1
            return b

        def proj_group(slot, wkey, sub, ti):
            s, w = TCS[ti]
            b = next_ps()
            mm_group(bank(b)[:, 0:w], [(slot[:, kc, sub * 128:(sub + 1) * 128], HM[:, kc, s:s + w]) for kc in range(KC)],
                     [wkey] + [hmk(kc, ti) for kc in range(KC)], [("ps", b)])
            return b

        slot, wkey = load_w("w_in", l, 0, 16, 0, 512)
        for sub in range(4):
            for ti in mixtc:
                s, w = TCS[ti]
                b = proj_group(slot, wkey, sub, ti)
                act(finT[:, sub, s:s + w], bank(b)[:, 0:w], AF.Copy, [("ps", b)], [("finT", sub, ti)])
        ntk = 10 if not last else 8
        for tk in range(ntk):
            ti = 2 if tk >= 8 else tk // 4
            pbt = tk % 2
            pbi = 3 if pbt == 0 else 0

            def fn(e, tk=tk, pbi=pbi):
                ins = None
                for g in range(4):
                    ins = e.matmul(PB[pbi][:, g * 256:(g + 1) * 256], lhsT=finT[:, g, tk * 128:(tk + 1) * 128],
                                   rhs=wAB[:, g, :], start=True, stop=True)
                return ins
            T.op("pe", [("finT", g, ti) for g in range(4)] + ["wAB"], [("ps", pbi * 2), ("ps", pbi * 2 + 1)], fn)
            if tk < 8:
                ub = tk % 2
                T.op("dve", [("ps", pbi * 2), ("ps", pbi * 2 + 1)], [("uast", ub)],
                     lambda e, pbi=pbi, ub=ub: e.tensor_copy(out=uast[:, ub, :], in_=PB[pbi][:]))
                T.dma("sp", [("uast", ub)], ["fin"], lambda e, tk=tk, ub=ub: e.dma_start(
                    out=fin[tk * 128:(tk + 1) * 128, :], in_=uast[:, ub, :]))
            else:
                T.op("dve", [("ps", pbi * 2), ("ps", pbi * 2 + 1)], ["uabc"],
                     lambda e, pbi=pbi, tk=tk: e.tensor_copy(out=uabc[:, tk - 8, :], in_=PB[pbi][:]))
        T.custom("pool", ["fin"], ["fout"], lambda e: e.collective_compute(
            "AllGather", ALU.bypass, replica_groups=PAIRS, ins=[fin.ap().opt()], outs=[fout.ap().opt()]), ("cc", l, 0), 1)
        for hf in range(2):
            slot, wkey = load_w("w_in", l, 0, 16, 1024 + hf * 256, 1024 + hf * 256 + 256)
            for sub in range(2):
                for ti in mixtc:
                    s, w = TCS[ti]
                    b = proj_group(slot, wkey, sub, ti)
                    act(sg[:, sub, s:s + w], bank(b)[:, 0:w], AF.Sigmoid, [("ps", b)], [("sg", sub, ti)])
            slot, wkey = load_w("w_in", l, 0, 16, 512 + hf * 256, 512 + hf * 256 + 256)
            for sub in range(2):
                ch = hf * 2 + sub
                for ti in mixtc:
                    s, w = TCS[ti]
                    b = proj_group(slot, wkey, sub, ti)
                    if ti < 2:
                        dst = zT[:, ch, 15 + s:15 + s + w]
                    else:
                        dst = zcT[:, ch, 15:15 + NCX]
                    tt("dve", dst, bank(b)[:, 0:w], sg[:, sub, s:s + w], ALU.mult,
                       [("ps", b), ("sg", sub, ti), "zhalo"], [("z", ch, ti)])
        for ch in range(4):
            T.dma("sp", [("z", ch, 0)], ["zin"], lambda e, ch=ch: e.dma_start(
                out=zin[ch * 128:(ch + 1) * 128, 0:15], in_=zT[:, ch, 15:30]))
            T.dma("sp", [("z", ch, 1)], ["zin"], lambda e, ch=ch: e.dma_start(
                out=zin[ch * 128:(ch + 1) * 128, 15:30], in_=zT[:, ch, NL:NL + 15]))
        T.custom("pool", ["zin"], ["zout"], lambda e: e.collective_compute(
            "AllGather", ALU.bypass, replica_groups=PAIRS, ins=[zin.ap().opt()], outs=[zout.ap().opt()]), ("cc", l, 1), 1)
        pend_rope = []
        rope_n = [0]
        for pc in range(3, 7):
            slot, wkey = load_w("w_in", l, 0, 16, pc * 512, (pc + 1) * 512)
            isq = pc < 5
            for sub in range(4):
                hd = (pc - 3) % 2 * 4 + sub
                tcl = mixtc if isq else alltc
                for ti in tcl:
                    s, w = TCS[ti]
                    b = proj_group(slot, wkey, sub, ti)
                    if ti == 2:
                        dst = QT[:, hd, NL:NT] if isq else KcT[:, hd, :]
                        act(dst, bank(b)[:, 0:w], AF.Copy, [("ps", b)], [("QT", hd, 2) if isq else ("KcT", hd)])
                        continue
                    rb = rope_n[0] % 2
                    rope_n[0] += 1
                    act(t32[:, rb, :], bank(b)[:, 0:w], AF.Copy, [("ps", b)], [("t32", rb)])
                    T.op("pool", [("t32", rb)], [("tb", rb)], lambda e, rb=rb: e.tensor_copy(out=tb[:, rb, :], in_=t32[:, rb, :]))
                    for f in pend_rope:
                        f()
                    pend_rope[:] = []

                    def tail(rb=rb, s=s, w=w, hd=hd, ti=ti, isq=isq):
                        mm_group(bank(6 + rb)[:, :], [(perm16[:], tb[:, rb, :])], [("tb", rb), "perm16"], [("ps", 6 + rb)])
                        tt("dve", t32[:, rb, :], t32[:, rb, :], rope[:, 0, s:s + w], ALU.mult, [("t32", rb), "rope"], [("t32", rb)])
                        tt("dve", vv[:, rb, :], bank(6 + rb)[:, :], rope[:, 1, s:s + w], ALU.mult, [("ps", 6 + rb), "rope"], [("vv", rb)])
                        if isq:
                            tt("pool", QT[:, hd, s:s + w], t32[:, rb, :], vv[:, rb, :], ALU.add, [("t32", rb), ("vv", rb)], [("QT", hd, ti)])
                        else:
                            tt("pool", kst[:, rb, :], t32[:, rb, :], vv[:, rb, :], ALU.add, [("t32", rb), ("vv", rb)], [("kst", rb)])
                            T.dma("sp", [("kst", rb)], ["kin"], lambda e, hd=hd, s=s, w=w, rb=rb: e.dma_start(
                                out=kin[hd * 128:(hd + 1) * 128, s:s + w], in_=kst[:, rb, :]))
                    pend_rope.append(tail)
        for f in pend_rope:
            f()
        pend_rope[:] = []
        T.custom("pool", ["kin"], ["kout"], lambda e: e.collective_compute(
            "AllGather", ALU.bypass, replica_groups=PAIRS, ins=[kin.ap().opt()], outs=[kout.ap().opt()]), ("cc", l, 2), 1)
        for pc in range(7, 9):
            slot, wkey = load_w("w_in", l, 0, 16, pc * 512, (pc + 1) * 512)
            for tk in range(10):
                ti = 2 if tk >= 8 else tk // 4
                b = next_ps()
                mm_group(bank(b)[:, :], [(HM[:, kc, tk * 128:(tk + 1) * 128], slot[:, kc, :]) for kc in range(KC)],
                         [wkey] + [hmk(kc, ti) for kc in range(KC)], [("ps", b)])
                if tk < 8:
                    vb = tk % 2
                    act(vst[:, vb, :], bank(b)[:, :], AF.Copy, [("ps", b)], [("vst", vb)])
                    T.dma("sp", [("vst", vb)], ["vin"], lambda e, tk=tk, pc=pc, vb=vb: e.dma_start(
                        out=vin[tk * 128:(tk + 1) * 128, (pc - 7) * 512:(pc - 6) * 512], in_=vst[:, vb, :]))
                else:
                    act(Vc[:, tk - 8, (pc - 7) * 512:(pc - 6) * 512], bank(b)[:, :], AF.Copy, [("ps", b)], ["Vc"])
        T.custom("pool", ["vin"], ["vout"], lambda e: e.collective_compute(
            "AllGather", ALU.bypass, replica_groups=PAIRS, ins=[vin.ap().opt()], outs=[vout.ap().opt()]), ("cc", l, 3), 1)
        zov = zout.ap().rearrange("(r c p) n -> p r c n", r=2, c=4)
        T.dma("sp", ["zout"], ["hl"], lambda e: e.dma_start(out=hl[:, :, 0, :], in_=zov[:, 0, :, 15:30]))
        T.dma("sp", ["zout"], ["hl"], lambda e: e.dma_start(out=hl[:, :, 1, :], in_=zov[:, 1, :, 0:15]))
        mo, _ = VLAY["mask"]
        ts(zT[:, :, 0:15], hl[:, :, 0, :], VEC[:, mo:mo + 1], None, ALU.mult, None, ["hl", "VEC", "zhalo"], ["zhl"])
        ts(zT[:, :, NL + 15:NL + 30], hl[:, :, 1, :], VEC[:, mo + 1:mo + 2], None, ALU.mult, None, ["hl", "VEC", "zhalo"], ["zhr"])
        if l == 0:
            dump("QT", QT[:].rearrange("p a b -> p (a b)"), [128, 8 * NT], BF16, [("QT", h, t) for h in range(8) for t in range(3)])
            dump("zT", zT[:].rearrange("p a b -> p (a b)"), [128, 4 * (NL + 30)], BF16, [("z", c, t) for c in range(4) for t in range(2)] + ["zhl", "zhr"])
            dump("KcT", KcT[:].rearrange("p a b -> p (a b)"), [128, 8 * NCX], BF16, [("KcT", h) for h in range(8)])
            dump("Vc", Vc[:].rearrange("p a b -> p (a b)"), [128, 2048], BF16, ["Vc"])
            dump("uabc", uabc[:].rearrange("p a b -> p (a b)"), [128, 2048], BF16, ["uabc"])
            if dbg:
                for nm, tsr, shp in (("kout", kout, [2048, NL]), ("vout", vout, [2 * NL, 1024]), ("fout", fout, [2 * NL, 1024])):
                    tdb = nc.dram_tensor("dbg_" + nm, shp, BF16, kind="ExternalOutput")
                    dbg_outs.append(T.dma("sp", [nm], [("dbg", nm)], lambda e, tdb=tdb, tsr=tsr: e.dma_start(out=tdb.ap(), in_=tsr.ap())))
        if stage <= 2:
            break
        T.barrier()
        wo, _ = VLAY["wdw_%d" % l]
        for ch in range(4):
            db = ch % 2
            for k in range(31):
                ts(Dg[:, db, k, :], ident16[:], VEC[:, wo + ch * 31 + k:wo + ch * 31 + k + 1], None, ALU.mult, None,
                   ["ident16", "VEC"], [("Dg", db, k)])
            for ti in mixtc:
                s_, w = TCS[ti]
                b = 2 + (ch * 3 + ti) % 4
                if ti < 2:
                    pairs = [(Dg[:, db, k, :], zT[:, ch, s_ + k:s_ + k + w]) for k in range(31)]
                else:
                    pairs = [(Dg[:, db, k, :], zcT[:, ch, k:k + w]) for k in range(31)]
                mm_group(bank(b)[:, 0:w], pairs, [("Dg", db, k) for k in range(31)], [("ps", b)])
                act(czf[:, ch, s_:s_ + w], bank(b)[:, 0:w], AF.Identity, [("ps", b), "VEC"], [("czf", ch, ti)],
                    bias=V("bdw_%d" % l, ch, ch + 1), scale=1.0)
        lo, hi = 0, (NL if last else NT)
        for ch in range(4):
            kk = [("czf", ch, ti) for ti in mixtc]
            act(sq[:, 0, lo:hi], czf[:, ch, lo:hi], AF.Copy, kk, [("sq", 0)])
            act(sq[:, 1, lo:hi], czf[:, ch, lo:hi], AF.Square, kk, [("sq", 1)])

            def fn(e, ch=ch, mixtc=mixtc):
                ins = None
                for ti in mixtc:
                    s, w = TCS[ti]
                    e.matmul(bank(ti)[:, 0:w], lhsT=ones16[:], rhs=sq[:, 0, s:s + w], start=(ch == 0), stop=(ch == 3))
                    ins = e.matmul(bank(3 + ti)[:, 0:w], lhsT=ones16[:], rhs=sq[:, 1, s:s + w], start=(ch == 0), stop=(ch == 3))
                return ins
            T.op("pe", [("sq", 0), ("sq", 1), "ones16"], [("ps", i) for i in range(6)], fn)
        for ti in mixtc:
            s, w = TCS[ti]
            act(mean[:, s:s + w], bank(ti)[:, 0:w], AF.Copy, [("ps", ti)], [("tn", 0)], scale=1.0 / 512)
            act(ex2[:, s:s + w], bank(3 + ti)[:, 0:w], AF.Copy, [("ps", 3 + ti)], [("tn", 1)], scale=1.0 / 512)
            tt("dve", crs[:, s:s + w], mean[:, s:s + w], mean[:, s:s + w], ALU.mult, [("tn", 0)], [("rs", ti)])
            tt("dve", ex2[:, s:s + w], ex2[:, s:s + w], crs[:, s:s + w], ALU.subtract, [("tn", 1), ("rs", ti)], [("tn", 1)])
            act(crs[:, s:s + w], ex2[:, s:s + w], AF.Ln, [("tn", 1)], [("rs", ti)], bias=EPS, scale=1.0)
            act(crs[:, s:s + w], crs[:, s:s + w], AF.Exp, [("rs", ti)], [("rs", ti)], scale=-0.5)
        for ch in range(4):
            for ti in mixtc:
                s, w = TCS[ti]
                tt("dve", czf[:, ch, s:s + w], czf[:, ch, s:s + w], mean[:, s:s + w], ALU.subtract,
                   [("czf", ch, ti), ("tn", 0)], [("czf", ch, ti)])
                tt("dve", czf[:, ch, s:s + w], czf[:, ch, s:s + w], crs[:, s:s + w], ALU.mult,
                   [("czf", ch, ti), ("rs", ti)], [("czf", ch, ti)])
                act(sz[:, ch, s:s + w], czf[:, ch, s:s + w], AF.Silu, [("czf", ch, ti), "VEC"], [("sz", ch, ti)],
                    bias=V("bln_%d" % l, ch, ch + 1), scale=V("gln_%d" % l, ch, ch + 1))
        T.dma("pool", [], ["wpw"], lambda e, l=l: e.dma_start(out=wpw[:], in_=w_pw_d[l].rearrange("(kc p) n -> p kc n", p=128)))
        for ec in range(4):
            for ti in mixtc:
                s, w = TCS[ti]
                b = 6 + (ec * 3 + ti) % 2
                mm_group(bank(b)[:, 0:w], [(wpw[:, kc, ec * 128:(ec + 1) * 128], sz[:, kc, s:s + w]) for kc in range(4)],
                         ["wpw"] + [("sz", kc, ti) for kc in range(4)], [("ps", b)])
                act(HM[:, 4 + ec, s:s + w], bank(b)[:, 0:w], AF.Identity, [("ps", b), "VEC"], [hmk(4 + ec, ti)],
                    bias=V("bpw_%d" % l, ec, ec + 1), scale=1.0)
        T.barrier()
        for j in range(16):
            ub = j % 3
            T.dma("sp", ["fout"], [("uabj", ub)], lambda e, j=j, ub=ub: e.dma_start(out=uabj[:, ub, :], in_=fout[j * 128:(j + 1) * 128, :]))
            T.dma("sp", [], [("tab", ub)], lambda e, j=j, ub=ub: e.dma_start(out=tab[:, ub, 0, :], in_=CL_d[j * 128:(j + 1) * 128, :]))
            T.dma("sp", [], [("tab", ub)], lambda e, j=j, ub=ub: e.dma_start(out=tab[:, ub, 1, :], in_=SL_d[j * 128:(j + 1) * 128, :]))

            def fn(e, j=j, ub=ub):
                ins = None
                for g in range(4):
                    for n in range(2):
                        e.matmul(bank(g * 2 + n)[:, :], lhsT=uabj[:, ub, g * 256:g * 256 + 128], rhs=tab[:, ub, 0, n * 512:(n + 1) * 512],
                                 start=(j == 0), stop=False)
                    for n in range(2):
                        ins = e.matmul(bank(g * 2 + n)[:, :], lhsT=uabj[:, ub, g * 256 + 128:g * 256 + 256],
                                       rhs=tab[:, ub, 1, n * 512:(n + 1) * 512], start=False, stop=(j == 15))
                return ins
            T.op("pe", [("uabj", ub), ("tab", ub)], [("ps", b_) for b_ in range(8)], fn)
        for g in range(4):
            for n in range(2):
                act(HM[:, g, n * 512:(n + 1) * 512], bank(g * 2 + n)[:, :], AF.Copy, [("ps", g * 2 + n)], [hmk(g, n)])
        if not last:
            for j in range(2):
                T.dma("sp", [], ["tabc"], lambda e, j=j: e.dma_start(out=tabc[:, j, 0, :], in_=Cc_d[j * 128:(j + 1) * 128, :]))
                T.dma("sp", [], ["tabc"], lambda e, j=j: e.dma_start(out=tabc[:, j, 1, :], in_=Sc_d[j * 128:(j + 1) * 128, :]))
            for g in range(4):
                pairs = []
                for j in range(2):
                    pairs.append((uabc[:, j, g * 256:g * 256 + 128], tabc[:, j, 0, :]))
                    pairs.append((uabc[:, j, g * 256 + 128:g * 256 + 256], tabc[:, j, 1, :]))
                b = 4 + g % 2
                mm_group(bank(b)[:, 0:NCX], pairs, ["uabc", "tabc"], [("ps", b)])
                act(HM[:, g, NL:NT], bank(b)[:, 0:NCX], AF.Copy, [("ps", b)], [hmk(g, 2)])
        kov = kout.ap().rearrange("(r h p) t -> p r h t", r=2, h=8)
        neglam = LAM[:, l * 4:l * 4 + 1]
        gsubs = LAM[:, l * 4 + 1:l * 4 + 2]
        it_n, s_n, p_n, q_n = [0], [0], [0], [0]
        pending = []
        pend_sum = []
        pend_pv = []

        def run_pending(upto):
            keep = []
            for st_, f in pending:
                if st_ <= upto:
                    f()
                else:
                    keep.append([st_, f])
            pending[:] = keep

        for hd in range(8):
            hb = hd % 2
            T.dma("sp", ["kout"], [("Kh", hb)], lambda e, hd=hd, hb=hb: e.dma_start(
                out=Kh[:, hb, :].rearrange("p (r t) -> p r t", r=2), in_=kov[:, :, hd, :]))
            T.dma("sp", ["vout"], [("Vh", hb)], lambda e, hd=hd, hb=hb: e.dma_start(
                out=Vh[:, hb, :, :], in_=vout.ap().rearrange("(kc p) e -> p kc e", p=128)[:, :, hd * 128:(hd + 1) * 128]))
            qlist = [(qs, list(range(18))) for qs in (0, 256, 512, 768)]
            if not last:
                qlist.append((NL, [0, 1]))
            for (qs, kcs) in qlist:
                qti = 2 if qs >= NL else qs // 512
                it = it_n[0]
                it_n[0] += 1
                par = it % 2
                ob, sbk = 4 + par, 6 + par
                nbk = sbk

                def k_ap(kc, m):
                    if kc < 2:
                        return KcT[m * 64:(m + 1) * 64, hd, kc * 128:(kc + 1) * 128]
                    return Kh[m * 64:(m + 1) * 64, hb, (kc - 2) * 128:(kc - 1) * 128]

                def v_ap(kc):
                    if kc < 2:
                        return Vc[:, kc, hd * 128:(hd + 1) * 128]
                    return Vh[:, hb, kc - 2, :]

                def kkeys(kc):
                    return [("KcT", hd)] if kc < 2 else [("Kh", hb)]

                def vkeys(kc):
                    return ["Vc"] if kc < 2 else [("Vh", hb)]

                def s_op(kc):
                    sb_ = s_n[0] % 2
                    s_n[0] += 1
                    aps = (PB[sb_][:, 0:256], k_ap(kc, 0), QT[0:64, hd, qs:qs + 256],
                           PB[sb_][:, 512:768], k_ap(kc, 1), QT[64:128, hd, qs:qs + 256])

                    def fn(e, aps=aps):
                        e.matmul(aps[0], lhsT=aps[1], rhs=aps[2], start=True, stop=True)
                        return e.matmul(aps[3], lhsT=aps[4], rhs=aps[5], start=True, stop=True)
                    T.op("pe", kkeys(kc) + [("QT", hd, qti)], [("ps", sb_ * 2), ("ps", sb_ * 2 + 1)], fn)
                    return sb_

                pend_s = s_op(kcs[0])
                for i, kc in enumerate(kcs):
                    sb_ = pend_s
                    pb_ = p_n[0] % 4
                    p_n[0] += 1
                    act(pt[:, pb_, :].rearrange("p (m q) -> p m q", m=2),
                        PB[sb_][:].rearrange("p (m q) -> p m q", m=2)[:, :, 0:256], AF.Exp,
                        [("ps", sb_ * 2), ("ps", sb_ * 2 + 1)], [("pt", pb_)], scale=0.125)
                    if i + 1 < len(kcs):
                        pend_s = s_op(kcs[i + 1])
                    aps = (v_ap(kc), pt[:, pb_, :], bank(ob)[:, :])

                    def fn(e, aps=aps, first=(i == 0), lastk=(i == len(kcs) - 1)):
                        return e.matmul(aps[2], lhsT=aps[0], rhs=aps[1], start=first, stop=lastk)
                    if pend_pv:
                        pend_pv.pop()()
                    pend_pv.append(lambda fn=fn, rk=vkeys(kc) + [("pt", pb_)], ob=ob: T.op("pe", rk, [("ps", ob)], fn))
                    if pend_sum:
                        pend_sum.pop()()
                    if i % 2 == 0:
                        pb_even = pb_
                    else:
                        qb_ = q_n[0] % 2
                        q_n[0] += 1
                        tt("dve", pts[:, qb_, :], pt[:, pb_even, :], pt[:, pb_, :], ALU.add,
                           [("pt", pb_even), ("pt", pb_)], [("pts", qb_)])
                        pend_sum.append(lambda qb_=qb_, sbk=sbk, first=(i == 1), lastp=(i == len(kcs) - 1): T.op(
                            "pe", [("pts", qb_), "ones16"], [("ps", sbk)], lambda e: e.matmul(
                                bank(sbk)[:, :], lhsT=ones16[:], rhs=pts[:, qb_, :], start=first, stop=lastp)))
                    run_pending(i)
                if pend_pv:
                    pend_pv.pop()()
                if pend_sum:
                    pend_sum.pop()()
                run_pending(10 ** 9)

                def mk(hd=hd, qs=qs, qti=qti, par=par, ob=ob, sbk=sbk, nbk=nbk):
                    def A1():
                        act(frA[:, par, :], bank(sbk)[:, :], AF.Ln, [("ps", sbk)], [("frA", par)])
                        act(frA[:, par, :], frA[:, par, :], AF.Exp, [("frA", par)], [("frA", par)], scale=-1.0)
                        tt("dve", fo2[:, par, :], bank(ob)[:, :], frA[:, par, :], ALU.mult, [("ps", ob), ("frA", par)], [("fo2", par)])
                        stt(fot[:, par, :], fo2[:, par, 256:512], neglam, fo2[:, par, 0:256], ALU.mult, ALU.add,
                            [("fo2", par), ("LAM", l, 0)], [("fot", par)])

                    def A2():
                        act(sqo[:, par, :], fot[:, par, :], AF.Square, [("fot", par)], [("sqo", par)])

                    def PEm():
                        mm_group(bank(nbk)[:, 0:256], [(ones16[:], sqo[:, par, :])], [("sqo", par), "ones16"], [("ps", nbk)])

                    def B1():
                        act(frB[:, par, :], bank(nbk)[:, 0:256], AF.Ln, [("ps", nbk)], [("frB", par)], bias=EPS, scale=1.0 / 128)
                        act(frB[:, par, :], frB[:, par, :], AF.Exp, [("frB", par)], [("frB", par)], scale=-0.5)

                    def B2():
                        tt("dve", fot[:, par, :], fot[:, par, :], frB[:, par, :], ALU.mult, [("fot", par), ("frB", par)], [("fot", par)])

                    def B3():
                        act(HM[:, 8 + hd, qs:qs + 256], fot[:, par, :], AF.Identity, [("fot", par), ("LAM", l, 1)],
                            [hmk(8 + hd, qti)], bias=0.0, scale=gsubs)
                    return A1, A2, PEm, B1, B2, B3
                A1, A2, PEm, B1, B2, B3 = mk()
                A1()
                pending.extend([[2, A2], [4, PEm], [5, B1], [7, B2], [8, B3]])
        run_pending(10 ** 9)
        if l == 0:
            dump("mix0", HM[:].rearrange("p a b -> p (a b)"), [128, KC * NT], BF16, [hmk(c, ti) for c in range(KC) for ti in range(3)])
        if stage <= 3:
            break
        T.barrier()
        rr = [0]
        pend_stat = []
        for pc in range(4):
            slot, wkey = load_w("w_out", l, 0, 16, pc * 512, (pc + 1) * 512)
            for sub in range(4):
                dc = pc * 4 + sub
                for ti in mixtc:
                    s, w = TCS[ti]
                    b = 3 + rr[0] % 3
                    qb = rr[0] % 2
                    rr[0] += 1
                    mm_group(bank(b)[:, 0:w], [(slot[:, kc, sub * 128:(sub + 1) * 128], HM[:, kc, s:s + w]) for kc in range(KC)],
                             [wkey] + [hmk(kc, ti) for kc in range(KC)], [("ps", b)])
                    for f in pend_stat:
                        f()
                    pend_stat = []
                    act(YB[:, dc, s:s + w], bank(b)[:, 0:w], AF.Copy, [("ps", b)], [ybk(dc, ti)])
                    act(sq[:, qb, 0:w], bank(b)[:, 0:w], AF.Square, [("ps", b)], [("sq", qb)])
                    pend_stat.append(lambda ti=ti, w=w, qb=qb, dc=dc: T.op(
                        "pe", [("sq", qb), "ones16"], [("ps", ti)], lambda e: e.matmul(
                            bank(ti)[:, 0:w], lhsT=ones16[:], rhs=sq[:, qb, 0:w], start=(dc == 0), stop=(dc == KC - 1))))
        for f in pend_stat:
            f()
        for ti in mixtc:
            s, w = TCS[ti]
            act(rs[:, s:s + w], bank(ti)[:, 0:w], AF.Ln, [("ps", ti)], [("rs", ti)], bias=EPS, scale=1.0 / D)
            act(rs[:, s:s + w], rs[:, s:s + w], AF.Exp, [("rs", ti)], [("rs", ti)], scale=-0.5)
        lo, hi = 0, (NL if last else NT)

        def resid(l, gate_which, final, stat_tcis=None):
            def load_x(dc):
                xb = dc % 2
                if l == 0 and not final:
                    T.dma("sp", [], [("tn", xb)], lambda e, dc=dc, xb=xb: e.dma_start(out=tn[:, xb, 0:NL], in_=xT_d[dc * 128:(dc + 1) * 128, :]))
                    T.dma("sp", [], [("tn", xb)], lambda e, dc=dc, xb=xb: e.dma_start(out=tn[:, xb, NL:NT], in_=cxT_d[dc * 128:(dc + 1) * 128, :]))
                else:
                    T.dma("sp", [("xs", dc)], [("tn", xb)], lambda e, dc=dc, xb=xb, lo=lo, hi=hi: e.dma_start(out=tn[:, xb, lo:hi], in_=xs_d.ap()[dc, :, lo:hi]))
            load_x(0)
            load_x(1)
            for dc in range(KC):
                xb = dc % 2
                kk = [ybk(dc, ti) for ti in mixtc]
                tt("dve", YB[:, dc, lo:hi], YB[:, dc, lo:hi], rs[:, lo:hi], ALU.mult, kk + [("rs", ti) for ti in mixtc], kk)
                stt(YB[:, dc, 0:NL], YB[:, dc, 0:NL], scal(l, gate_which, dc, 0), tn[:, xb, 0:NL], ALU.mult, ALU.add,
                    [ybk(dc, 0), ybk(dc, 1), ("tn", xb), ("DER", l)], [ybk(dc, 0), ybk(dc, 1)])
                if not last:
                    stt(YB[:, dc, NL:NT], YB[:, dc, NL:NT], scal(l, gate_which, dc, 1), tn[:, xb, NL:NT], ALU.mult, ALU.add,
                        [ybk(dc, 2), ("tn", xb), ("DER", l)], [ybk(dc, 2)])
                if final and last:
                    fin_toks.append(T.dma("sp", [ybk(dc, 0), ybk(dc, 1)], [("out", dc)], lambda e, dc=dc: e.dma_start(
                        out=out_d[dc * 128:(dc + 1) * 128, :], in_=YB[:, dc, 0:NL])))
                else:
                    T.dma("sp", kk, [("xs", dc)], lambda e, dc=dc, lo=lo, hi=hi: e.dma_start(out=xs_d.ap()[dc, :, lo:hi], in_=YB[:, dc, lo:hi]))
                if dc + 2 < KC:
                    load_x(dc + 2)
                if stat_tcis is not None:
                    stats_chunk(dc, YB[:, dc, :], [ybk(dc, ti) for ti in stat_tcis], stat_tcis, KC, 0)
            if stat_tcis is not None:
                rstd_finish(stat_tcis, rs, "rs", D, 0)

        resid(l, 1, False, stat_tcis=mixtc)
        norm_to_HM(l, 2, "bB", mixtc)
        if l == 0:
            dump("x1", YB[:].rearrange("p a b -> p (a b)"), [128, KC * NT], F32, [ybk(c, ti) for c in range(KC) for ti in range(3)])
        if stage <= 4:
            break
        w1v = w_m1_d[l].rearrange("(kc p) n -> p kc n", p=128)
        w2v = w_m2_d[l].rearrange("(kc p) n -> p kc n", p=128)
        rr = [0]
        for fb in range(8):
            for pi in range(2):
                slot, wkey = load_w("w_m1", l, 0, 16, fb * 1024 + pi * 512, fb * 1024 + pi * 512 + 512)
                for sub in range(4):
                    hc = pi * 4 + sub
                    for ti in mixtc:
                        s, w = TCS[ti]
                        b = rr[0] % 4
                        rb = rr[0] % 2
                        rr[0] += 1
                        mm_group(bank(b)[:, 0:w], [(slot[:, kc, sub * 128:(sub + 1) * 128], HM[:, kc, s:s + w]) for kc in range(KC)],
                                 [wkey] + [hmk(kc, ti) for kc in range(KC)], [("ps", b)])
                        ts(rl[:, rb, 0:w], bank(b)[:, 0:w], 0.0, None, ALU.max, None, [("ps", b)], [("rl", rb)])
                        act(hid[:, hc, s:s + w], rl[:, rb, 0:w], AF.Square, [("rl", rb)], [("hid", hc, ti)])
            for pj in range(4):
                slot, wkey = load_w("w_m2", l, fb * 8, (fb + 1) * 8, pj * 512, (pj + 1) * 512)
                for sub in range(4):
                    dc = pj * 4 + sub
                    for ti in mixtc:
                        s, w = TCS[ti]
                        b = 4 + rr[0] % 4
                        rr[0] += 1
                        mm_group(bank(b)[:, 0:w], [(slot[:, kc, sub * 128:(sub + 1) * 128], hid[:, kc, s:s + w]) for kc in range(8)],
                                 [wkey] + [("hid", kc, ti) for kc in range(8)], [("ps", b)])
                        if fb == 0:
                            act(YB[:, dc, s:s + w], bank(b)[:, 0:w], AF.Copy, [("ps", b)], [ybk(dc, ti)])
                        else:
                            tt("dve", YB[:, dc, s:s + w], YB[:, dc, s:s + w], bank(b)[:, 0:w], ALU.add, [("ps", b), ybk(dc, ti)], [ybk(dc, ti)])
        sumsq_rstd(lambda c: YB[:, c, :], lambda c: [ybk(c, ti) for ti in mixtc], mixtc, KC, rs, "rs", D, 0)
        resid(l, 3, True, stat_tcis=(None if last else alltc))
        if l == 0:
            dump("x2", YB[:].rearrange("p a b -> p (a b)"), [128, KC * NT], F32, [ybk(c, ti) for c in range(KC) for ti in range(3)])
        if stage <= 5:
            break
        if not last:
            T.barrier()

    if not fin_toks:
        fin_toks.append(T.dma("sp", [], [("out", 0)], lambda e: e.dma_start(out=out_d[0:128, :], in_=tn[:, 0, 0:NL])))
    if wseq is None:
        return wrec
    T.final_wait("sp", fin_toks + dbg_outs)
    with contextlib.ExitStack() as st:
        T.emit(nc, st)
    return nc


def host_inputs(inp):
    f32 = np.float32
    x = np.asarray(inp["x"], f32)
    ctx = np.asarray(inp["ctx"], f32)
    c = np.asarray(inp["c"], f32)
    c_ctx = np.asarray(inp["c_ctx"], f32)
    shared = {
        "w_in": np.ascontiguousarray(inp["w_in"], dtype=f32),
        "w_out": np.ascontiguousarray(inp["w_out"], dtype=f32),
        "w_f": np.ascontiguousarray(np.transpose(np.asarray(inp["w_fourier"], f32), (0, 2, 1, 3))),
        "w_pw": np.ascontiguousarray(inp["w_conv_pw"], dtype=f32),
        "w_m1": np.ascontiguousarray(inp["w_mlp_in"], dtype=f32),
        "w_m2": np.ascontiguousarray(inp["w_mlp_out"], dtype=f32),
    }
    k = np.arange(128)
    ang = 2 * np.pi * np.outer(k, k) / 128.0
    cs_ = np.concatenate([np.cos(ang), np.sin(ang)], axis=1).astype(f32)
    cs_hi = cs_.astype(ml_dtypes.bfloat16)
    cs_lo = (cs_ - cs_hi.astype(f32)).astype(ml_dtypes.bfloat16)
    shared["dftc"] = np.ascontiguousarray(np.concatenate([cs_hi, cs_lo], axis=1))
    kk = np.arange(NCX)
    angc = 2 * np.pi * (np.outer(kk, kk) % NCX) / NCX
    nrm_c = 1.0 / math.sqrt(NCX * 128)
    shared["Cc"] = (np.cos(angc) * nrm_c).astype(f32).astype(ml_dtypes.bfloat16)
    shared["Scn"] = (-np.sin(angc) * nrm_c).astype(f32).astype(ml_dtypes.bfloat16)
    perm = np.zeros((128, 128), f32)
    for p in range(128):
        j = p % 64
        q = p + 32 if j < 32 else p - 32
        perm[q, p] = 1.0
    shared["perm"] = perm
    shared["ident"] = np.eye(128, dtype=f32)
    n_freq = 16
    inv = (10000.0 ** (-np.arange(n_freq, dtype=np.float64) / n_freq))
    L = 2048
    nrm = 1.0 / math.sqrt(L * 128)
    half_tabs = []
    for half in range(2):
        idx = np.arange(half * NL, (half + 1) * NL)
        row = (idx // 64).astype(np.float64)
        col = (idx % 64).astype(np.float64)
        a = np.concatenate([row[:, None] * inv, col[:, None] * inv], axis=-1)
        a = a.astype(np.float32).astype(np.float64)
        cs, sn = np.cos(a), np.sin(a)
        cosT = np.zeros((128, NL))
        sinT = np.zeros((128, NL))
        for p in range(128):
            j = p % 64
            cosT[p] = cs[:, j % 32]
            sinT[p] = -sn[:, j] if j < 32 else sn[:, j - 32]
        rope = np.stack([cosT, sinT], axis=1).astype(f32)
        jj = np.arange(L)
        angL = 2 * np.pi * ((np.outer(jj, idx)) % L) / L
        CL = (np.cos(angL) * nrm).astype(f32).astype(ml_dtypes.bfloat16)
        SLn = (-np.sin(angL) * nrm).astype(f32).astype(ml_dtypes.bfloat16)
        half_tabs.append((rope, CL, SLn))
    maps = []
    for r in range(8):
        b, half = r // 2, r % 2
        vec = np.zeros((128, NV), f32)

        def put(name, arr):
            o, w = VLAY[name]
            assert arr.shape == (128, w), (name, arr.shape, w)
            vec[:, o:o + w] = arr
        for l in range(DEPTH):
            put("badap_%d" % l, np.ascontiguousarray(np.repeat(chunked(inp["b_ada"][l])[:, r * 12:(r + 1) * 12], 5, axis=1)))
            put("gpm2_%d" % l, dup2(chunked(inp["g_pre_mix"][l])))
            put("gqm2_%d" % l, dup2(chunked(inp["g_post_mix"][l])))
            put("gpf2_%d" % l, dup2(chunked(inp["g_pre_mlp"][l])))
            put("gqf2_%d" % l, dup2(chunked(inp["g_post_mlp"][l])))
            wdw = np.asarray(inp["w_dw"][l], f32)
            put("wdw_%d" % l, np.ascontiguousarray(wdw.T.reshape(4, 128, 31).transpose(1, 0, 2).reshape(128, 124)))
            put("bdw_%d" % l, chunked(inp["b_dw"][l]))
            put("gln_%d" % l, chunked(inp["g_conv_ln"][l]))
            put("bln_%d" % l, chunked(inp["b_conv_ln"][l]))
            put("bpw_%d" % l, chunked(inp["b_conv_pw"][l]))
            put("gsub_%d" % l, chunked(inp["g_subln"][l]))
            for nm, key in (("lq1", "lambda_q1"), ("lk1", "lambda_k1"), ("lq2", "lambda_q2"), ("lk2", "lambda_k2")):
                put("%s_%d" % (nm, l), np.broadcast_to(np.asarray(inp[key][l], f32)[None, :], (128, 64)))
        cv = np.stack([chunked(c[i]) for i in range(4)] + [chunked(c_ctx)], axis=2).reshape(128, 80)
        put("cvec5", cv)
        sel = np.zeros((128, 4), f32)
        sel[:, b] = 1.0
        put("bsel", sel)
        put("mask", np.broadcast_to(np.array([[1.0 if half == 1 else 0.0, 1.0 if half == 0 else 0.0]], f32), (128, 2)))
        rope, CL, SLn = half_tabs[half]
        m = dict(shared)
        m["w_ada"] = np.ascontiguousarray(np.asarray(inp["w_ada"], f32)[:, :, r * 1536:(r + 1) * 1536])
        m["xT"] = np.ascontiguousarray(x[b, half * NL:(half + 1) * NL, :].T)
        m["cxT"] = np.ascontiguousarray(ctx[b].T)
        m["vec"] = vec
        m["rope"] = rope
        m["CL"] = CL
        m["SLn"] = SLn
        maps.append(m)
    return maps


_NC_CACHE = {}


def kernel(**inputs):
    stage = int(os.environ.get("KSTAGE", "99"))
    dbg = os.environ.get("KDBG", "0") == "1"
    key = (stage, dbg)
    if key not in _NC_CACHE:
        rec = build(stage, dbg, None)
        _NC_CACHE[key] = build(stage, dbg, rec)
    nc = _NC_CACHE[key]
    maps = host_inputs(inputs)
    res = run_bass_kernel_spmd(nc, maps, core_ids=list(range(8)))
    out = np.empty((4, 2048, D), np.float32)
    for r in range(8):
        b, half = r // 2, r % 2
        out[b, half * NL:(half + 1) * NL, :] = res.results[r]["outT"].T
    if dbg:
        kernel.last_results = res.results
    return out
```

```python
import os, math, contextlib
import numpy as np
import ml_dtypes
import concourse.bass as bass
import concourse.mybir as mybir
from concourse.bass_utils import run_bass_kernel_spmd

F32 = mybir.dt.float32
BF16 = mybir.dt.bfloat16
AF = mybir.ActivationFunctionType
ALU = mybir.AluOpType
AX = mybir.AxisListType

D = 2048
NL = 1024
NCX = 256
NT = NL + NCX
KC = 16
DEPTH = 2
EPS = 1e-6
IN_COLS = 4608
PAIRS = [[0, 1], [2, 3], [4, 5], [6, 7]]
TCS = [(0, 512), (512, 512), (1024, 256)]


class Tracker:
    ENGS = ("pe", "act", "dve", "pool", "sp")
    NDMA = {"sp": 8, "pool": 8, "act": 4}

    def __init__(self):
        self.ops = {e: [] for e in self.ENGS}
        self.count = {e: 0 for e in self.ENGS}
        self.last_write = {}
        self.readers = {}
        self.known = {e: {} for e in self.ENGS}
        self.dma_n = {q: 0 for q in self.NDMA}
        self.dma_latest = {}
        self.extra_sems = []

    def _add(self, waits, tok):
        if tok is None:
            return
        k, v = tok
        if waits.get(k, 0) < v:
            waits[k] = v

    def _deps(self, reads, writes):
        waits = {}
        for k in reads:
            self._add(waits, self.last_write.get(k))
        for k in writes:
            self._add(waits, self.last_write.get(k))
            for t in self.readers.get(k, ()):
                self._add(waits, t)
        return waits

    def _filter(self, eng, waits):
        out = []
        kn = self.known[eng]
        for k, v in waits.items():
            if k == eng and eng == "pe":
                continue
            if kn.get(k, 0) >= v:
                continue
            kn[k] = v
            out.append((k, v))
        return out

    def _record(self, tok, reads, writes):
        for k in reads:
            self.readers.setdefault(k, []).append(tok)
        for k in writes:
            self.last_write[k] = tok
            self.readers[k] = []

    def op(self, eng, reads, writes, fn):
        waits = self._filter(eng, self._deps(reads, writes))
        self.count[eng] += 1
        tok = (eng, self.count[eng])
        self.ops[eng].append((waits, fn, eng, 1))
        self._record(tok, reads, writes)
        return tok

    def dma(self, q, reads, writes, fn):
        n = self.dma_n[q]
        ns = self.NDMA[q]
        key = ("dma", q, n % ns)
        waits = self._deps(reads, writes)
        prev = 16 * (n // ns)
        if prev > 0:
            self._add(waits, (key, prev))
        waits = self._filter(q, waits)
        self.dma_n[q] = n + 1
        tok = (key, prev + 16)
        self.dma_latest[key] = prev + 16
        self.ops[q].append((waits, fn, key, 16))
        self._record(tok, reads, writes)
        return tok

    def custom(self, eng, reads, writes, fn, key, amt):
        waits = self._filter(eng, self._deps(reads, writes))
        if key not in self.extra_sems:
            self.extra_sems.append(key)
        v = self.dma_latest.get(key, 0) + amt
        self.dma_latest[key] = v
        tok = (key, v)
        self.ops[eng].append((waits, fn, key, amt))
        self._record(tok, reads, writes)
        return tok

    def barrier(self):
        toks = [(e, self.count[e]) for e in self.ENGS if self.count[e] > 0]
        toks += list(self.dma_latest.items())
        for e in self.ENGS:
            waits = {}
            for t in toks:
                self._add(waits, t)
            waits = self._filter(e, waits)
            if waits:
                self.ops[e].append((waits, None, None, 0))
        self.last_write = {}
        self.readers = {}

    def final_wait(self, eng, toks):
        waits = {}
        for t in toks:
            self._add(waits, t)
        waits = self._filter(eng, waits)
        self.ops[eng].append((waits, None, None, 0))

    def emit(self, nc, stack):
        sems = {}
        for e in self.ENGS:
            sems[e] = stack.enter_context(nc.semaphore("s_" + e))
        for q, ns in self.NDMA.items():
            for j in range(ns):
                sems[("dma", q, j)] = stack.enter_context(nc.semaphore("d_%s%d" % (q, j)))
        for i, k in enumerate(self.extra_sems):
            sems[k] = stack.enter_context(nc.semaphore("x%d" % i))
        block = stack.enter_context(nc.Block())

        def run(eng_name):
            def body(eng):
                for waits, fn, inc_key, amt in self.ops[eng_name]:
                    for k, v in waits:
                        eng.wait_ge(sems[k], v)
                    if fn is not None:
                        ins = fn(eng)
                        ins.then_inc(sems[inc_key], amt)
            return body

        block.tensor(run("pe"))
        block.scalar(run("act"))
        block.vector(run("dve"))
        block.gpsimd(run("pool"))
        block.sync(run("sp"))


def vec_layout():
    lay = {}
    off = 0

    def add(name, w):
        nonlocal off
        lay[name] = (off, w)
        off += w

    for l in range(DEPTH):
        add("badap_%d" % l, 60)
        for g in ("gpm", "gqm", "gpf", "gqf"):
            add("%s2_%d" % (g, l), 32)
        add("wdw_%d" % l, 124)
        for g in ("bdw", "gln", "bln", "bpw"):
            add("%s_%d" % (g, l), 4)
        add("gsub_%d" % l, 1)
        for g in ("lq1", "lk1", "lq2", "lk2"):
            add("%s_%d" % (g, l), 64)
    add("cvec5", 80)
    add("bsel", 4)
    add("mask", 2)
    return lay, off


VLAY, NV = vec_layout()


def chunked(v):
    v = np.asarray(v, np.float32)
    return np.ascontiguousarray(v.reshape(-1, 128).T)


def dup2(a):
    return np.ascontiguousarray(np.repeat(a, 2, axis=1))


def build(stage=99, dbg=False, wseq=None):
    nc = bass.Bass("TRN2", target_bir_lowering=False)
    T = Tracker()

    def din(name, shape, dt=F32):
        return nc.dram_tensor(name, list(shape), dt, kind="ExternalInput").ap()

    xT_d = din("xT", [D, NL])
    cxT_d = din("cxT", [D, NCX])
    vec_d = din("vec", [128, NV])
    w_ada_d = din("w_ada", [DEPTH, D, 1536])
    w_in_d = din("w_in", [DEPTH, D, IN_COLS])
    w_out_d = din("w_out", [DEPTH, D, D])
    w_f_d = din("w_f", [DEPTH, 128, 4, 128])
    w_pw_d = din("w_pw", [DEPTH, 512, 512])
    w_m1_d = din("w_m1", [DEPTH, D, 4 * D])
    w_m2_d = din("w_m2", [DEPTH, 4 * D, D])
    dft_d = din("dftc", [128, 512], BF16)
    CL_d = din("CL", [2048, NL], BF16)
    SL_d = din("SLn", [2048, NL], BF16)
    Cc_d = din("Cc", [NCX, NCX], BF16)
    Sc_d = din("Scn", [NCX, NCX], BF16)
    rope_d = din("rope", [128, 2, NL])
    perm_d = din("perm", [128, 128])
    ident_d = din("ident", [128, 128])
    out_d = nc.dram_tensor("outT", [D, NL], F32, kind="ExternalOutput").ap()

    xs_d = nc.dram_tensor("xs", [KC, 128, NT], F32)
    kin = nc.dram_tensor("kin", [1024, NL], BF16)
    kout = nc.dram_tensor("kout", [2048, NL], BF16)
    vin = nc.dram_tensor("vin", [NL, 1024], BF16)
    vout = nc.dram_tensor("vout", [2 * NL, 1024], BF16)
    fin = nc.dram_tensor("fin", [NL, 1024], BF16)
    fout = nc.dram_tensor("fout", [2 * NL, 1024], BF16)
    zin = nc.dram_tensor("zin", [512, 30], BF16)
    zout = nc.dram_tensor("zout", [1024, 30], BF16)
    gin = nc.dram_tensor("gin", [128, 120], F32)
    gout = nc.dram_tensor("gout", [1024, 120], F32)

    dbg_outs = []

    BASE, LIMIT = 16512, 229344

    class Arena:
        def __init__(self, start, limit):
            self.off, self.limit = start, limit

        def take(self, name, shape, dt):
            nb = int(np.prod(shape[1:])) * (4 if dt == F32 else 2)
            nb = (nb + 63) // 64 * 64
            assert self.off + nb <= self.limit, (name, self.off, nb, self.limit)
            t = nc.alloc_sbuf_tensor_at(name, list(shape), dt, offset=self.off)
            self.off += nb
            return t

    P = Arena(BASE, LIMIT)
    YB = P.take("YB", [128, KC, NT], F32)
    yb_off = BASE
    HM = P.take("HM", [128, KC, NT], BF16)
    WB = [P.take("WB%d" % i, [128, 8192], BF16) for i in range(2)]
    VEC = P.take("VEC", [128, NV], F32)
    MOD = P.take("MOD", [128, 2, 96, 2], F32)
    DER = P.take("DER", [128, 2, 4, 32], F32)
    LAM = P.take("LAM", [128, 16], F32)
    LT = P.take("LT", [128, 64], F32)
    ones16 = P.take("ones16", [128, 128], BF16)
    perm16 = P.take("perm16", [128, 128], BF16)
    ident16 = P.take("ident16", [128, 128], BF16)
    sc16 = P.take("sc16", [128, 80], BF16)
    GP = P.take("GP", [128, 2, 12, 5], F32)
    rs = P.take("rs", [128, NT], F32)
    sq = P.take("sq", [128, 2, NT], BF16)
    tn = P.take("tn", [128, 2, NT], F32)
    f_start = P.off
    Fm = Arena(f_start, LIMIT)
    hid = Fm.take("hid", [128, 8, NT], BF16)
    rl = Fm.take("rl", [128, 2, 512], F32)
    R1 = Arena(yb_off, yb_off + KC * NT * 4)
    QT = R1.take("QT", [128, 8, NT], BF16)
    zT = R1.take("zT", [128, 4, NL + 30], BF16)
    zcT = R1.take("zcT", [128, 4, NCX + 30], BF16)
    KcT = R1.take("KcT", [128, 8, NCX], BF16)
    Vc = R1.take("Vc", [128, 2, 1024], BF16)
    uabc = R1.take("uabc", [128, 2, 1024], BF16)
    r2_start = R1.off
    YB_END = yb_off + KC * NT * 4

    class Arena2:
        def __init__(self):
            self.segs = [[r2_start, YB_END], [f_start, LIMIT]]

        def take(self, name, shape, dt):
            nb = int(np.prod(shape[1:])) * (4 if dt == F32 else 2)
            nb = (nb + 63) // 64 * 64
            for s in self.segs:
                if s[0] + nb <= s[1]:
                    t = nc.alloc_sbuf_tensor_at(name, list(shape), dt, offset=s[0])
                    s[0] += nb
                    return t
            raise AssertionError(("R2 overflow", name, nb, self.segs))

    A2 = Arena2()
    sg = A2.take("sg", [128, 2, NT], F32)
    finT = A2.take("finT", [128, 4, NT], BF16)
    rope = A2.take("rope", [128, 2, NL], F32)
    t32 = A2.take("t32", [128, 2, 512], F32)
    tb = A2.take("tb", [128, 2, 512], BF16)
    vv = A2.take("vv", [128, 2, 512], F32)
    kst = A2.take("kst", [128, 2, 512], BF16)
    vst = A2.take("vst", [128, 2, 512], BF16)
    uast = A2.take("uast", [128, 2, 1024], BF16)
    hl = A2.take("hl", [128, 4, 2, 15], BF16)
    wAB = A2.take("wAB", [128, 4, 256], BF16)
    wf32 = A2.take("wf32", [128, 4, 128], F32)
    dft32 = A2.take("dft32", [128, 512], BF16)
    wfh = A2.take("wfh", [128, 4, 128], BF16)
    wfl = A2.take("wfl", [128, 4, 128], BF16)
    A3 = Arena2()
    czf = A3.take("czf", [128, 4, NT], F32)
    sz = A3.take("sz", [128, 4, NT], BF16)
    wpw = A3.take("wpw", [128, 4, 512], BF16)
    Dg = A3.take("Dg", [128, 2, 31, 128], BF16)
    mean = tn[:, 0, :]
    ex2 = tn[:, 1, :]
    crs = rs
    A4 = Arena2()
    Kh = A4.take("Kh", [128, 2, 2048], BF16)
    Vh = A4.take("Vh", [128, 2, 16, 128], BF16)
    pt = A4.take("pt", [128, 4, 512], BF16)
    pts = A4.take("pts", [128, 2, 512], BF16)
    frA = A4.take("frA", [128, 2, 512], F32)
    fo2 = A4.take("fo2", [128, 2, 512], F32)
    fot = A4.take("fot", [128, 2, 256], F32)
    sqo = A4.take("sqo", [128, 2, 256], BF16)
    frB = A4.take("frB", [128, 2, 256], F32)
    uabj = A4.take("uabj", [128, 3, 1024], BF16)
    tab = A4.take("tab", [128, 3, 2, 1024], BF16)
    tabc = A4.take("tabc", [128, 2, 2, 256], BF16)

    PB = [nc.alloc_psum_tensor("pb%d" % i, [128, 1024], F32) for i in range(4)]

    def bank(b):
        return PB[b // 2][:, (b % 2) * 512:(b % 2) * 512 + 512]

    def V(name, lo=0, hi=None):
        o, w = VLAY[name]
        hi = w if hi is None else hi
        return VEC[:, o + lo:o + hi]

    def dump(name, ap_src, shape, dt, reads):
        if not dbg:
            return
        t = nc.dram_tensor("dbg_" + name, list(shape), dt, kind="ExternalOutput")
        dbg_outs.append(T.dma("sp", reads, [("dbg", name)], lambda e: e.dma_start(out=t.ap(), in_=ap_src)))

    wb_n = [0]
    wb_issued = [0]
    wrec = []
    WD = {"w_ada": w_ada_d, "w_in": w_in_d, "w_out": w_out_d, "w_m1": w_m1_d, "w_m2": w_m2_d}

    def _wview(k, desc):
        _, _, kc0, kc1, c0, c1 = desc
        a, b = kc1 - kc0, c1 - c0
        return WB[k % 2][:, 0:a * b].rearrange("p (a b) -> p a b", a=a)

    def _issue_w(k):
        desc = wseq[k]
        wname, l_, kc0, kc1, c0, c1 = desc
        src_ap = WD[wname][l_].rearrange("(kc p) n -> p kc n", p=128)[:, kc0:kc1, c0:c1]
        view = _wview(k, desc)
        T.dma("pool", [], [("WB", k % 2)], lambda e: e.dma_start(out=view, in_=src_ap))

    def load_w(*desc):
        k = wb_n[0]
        wb_n[0] += 1
        view = _wview(k, desc)
        if wseq is None:
            wrec.append(tuple(desc))
            return view, ("WB", k % 2)
        assert tuple(wseq[k]) == tuple(desc), (k, wseq[k], desc)
        while wb_issued[0] <= min(k + 1, len(wseq) - 1):
            _issue_w(wb_issued[0])
            wb_issued[0] += 1
        return view, ("WB", k % 2)

    def mm_group(out_ap, pairs, reads, writes):
        def fn(e):
            n = len(pairs)
            ins = None
            for i, (l, r) in enumerate(pairs):
                ins = e.matmul(out_ap, lhsT=l, rhs=r, start=(i == 0), stop=(i == n - 1))
            return ins
        return T.op("pe", reads, writes, fn)

    def act(out, in_, func, reads, writes, bias=0.0, scale=1.0):
        return T.op("act", reads, writes,
                    lambda e: e.activation(out=out, in_=in_, func=func, bias=bias, scale=scale))

    def tt(eng, out, in0, in1, op, reads, writes):
        return T.op(eng, reads, writes, lambda e: e.tensor_tensor(out=out, in0=in0, in1=in1, op=op))

    def stt(out, in0, scalar, in1, op0, op1, reads, writes):
        return T.op("dve", reads, writes, lambda e: e.scalar_tensor_tensor(
            out=out, in0=in0, scalar=scalar, in1=in1, op0=op0, op1=op1))

    def ts(out, in0, s1, s2, op0, op1, reads, writes, eng="dve"):
        if s2 is None:
            return T.op(eng, reads, writes, lambda e: e.tensor_scalar(
                out=out, in0=in0, scalar1=s1, scalar2=None, op0=op0))
        return T.op(eng, reads, writes, lambda e: e.tensor_scalar(
            out=out, in0=in0, scalar1=s1, scalar2=s2, op0=op0, op1=op1))

    def hmk(c, ti):
        return ("HM", c, ti)

    def ybk(c, ti):
        return ("YB", c, ti)

    T.dma("sp", [], ["VEC"], lambda e: e.dma_start(out=VEC[:], in_=vec_d[:, :]))
    T.dma("pool", [], ["perm16"], lambda e: e.dma_start(out=perm16[:], in_=perm_d[:, :]))
    T.dma("pool", [], ["ident16"], lambda e: e.dma_start(out=ident16[:], in_=ident_d[:, :]))
    T.op("dve", [], ["ones16"], lambda e: e.memset(ones16[:], 1.0))
    for c in range(KC):
        T.dma("sp", [], [ybk(c, 0), ybk(c, 1)], lambda e, c=c: e.dma_start(
            out=YB[:, c, 0:NL], in_=xT_d[c * 128:(c + 1) * 128, :]))
        T.dma("sp", [], [ybk(c, 2)], lambda e, c=c: e.dma_start(
            out=YB[:, c, NL:NT], in_=cxT_d[c * 128:(c + 1) * 128, :]))
    def stats_chunk(c, src_ap, src_keys, tcis, nfeat_chunks, psb):
        b = c % 2
        lo, hi = TCS[tcis[0]][0], TCS[tcis[-1]][0] + TCS[tcis[-1]][1]
        act(sq[:, b, lo:hi], src_ap[:, lo:hi], AF.Square, src_keys, [("sq", b)])

        def fn(e, c=c, b=b):
            ins = None
            for ti in tcis:
                s, w = TCS[ti]
                ins = e.matmul(bank(psb + ti)[:, 0:w], lhsT=ones16[:], rhs=sq[:, b, s:s + w],
                               start=(c == 0), stop=(c == nfeat_chunks - 1))
            return ins
        T.op("pe", [("sq", b), "ones16"], [("ps", psb + ti) for ti in tcis], fn)

    def rstd_finish(tcis, out_rs, rs_key, div, psb):
        for ti in tcis:
            s, w = TCS[ti]
            act(out_rs[:, s:s + w], bank(psb + ti)[:, 0:w], AF.Ln, [("ps", psb + ti)], [(rs_key, ti)], bias=EPS, scale=1.0 / div)
            act(out_rs[:, s:s + w], out_rs[:, s:s + w], AF.Exp, [(rs_key, ti)], [(rs_key, ti)], scale=-0.5)

    def sumsq_rstd(src_fn, src_keys_fn, tcis, nfeat_chunks, out_rs, rs_key, div, psb):
        for c in range(nfeat_chunks):
            stats_chunk(c, src_fn(c), src_keys_fn(c), tcis, nfeat_chunks, psb)
        rstd_finish(tcis, out_rs, rs_key, div, psb)

    act(sc16[:], V("cvec5"), AF.Silu, ["VEC"], ["sc16"])
    sumsq_rstd(lambda c: YB[:, c, :], lambda c: [ybk(c, 0), ybk(c, 1), ybk(c, 2)], [0, 1, 2], KC, rs, "rs", D, 2)
    for l in range(DEPTH):
        for pc in range(3):
            slot, wkey = load_w("w_ada", l, 0, 16, pc * 512, (pc + 1) * 512)
            pbank = pc % 2
            psA = bank(pbank)

            def fn(e, slot=slot, psA=psA):
                ins = None
                for sub in range(4):
                    for kc in range(KC):
                        ins = e.matmul(psA[:, sub * 5:sub * 5 + 5], lhsT=slot[:, kc, sub * 128:(sub + 1) * 128],
                                       rhs=sc16[:, kc * 5:kc * 5 + 5], start=(kc == 0), stop=(kc == KC - 1))
                return ins
            T.op("pe", [wkey, "sc16"], [("ps", pbank)], fn)
            o, _ = VLAY["badap_%d" % l]
            tt("dve", GP[:, l, pc * 4:(pc + 1) * 4, :].rearrange("p a b -> p (a b)"), psA[:, 0:20],
               VEC[:, o + pc * 20:o + pc * 20 + 20], ALU.add, [("ps", pbank), "VEC"], ["GP"])
    T.dma("sp", ["GP"], ["gin"], lambda e: e.dma_start(out=gin[:, :], in_=GP[:].rearrange("p a b c -> p (a b c)")))
    T.custom("pool", ["gin"], ["gout"], lambda e: e.collective_compute(
        "AllGather", ALU.bypass, replica_groups=[list(range(8))], ins=[gin.ap().opt()], outs=[gout.ap().opt()]), ("ccg",), 1)
    Gt = tn[:, 0, 0:960]
    T.dma("sp", ["gout"], [("tn", 0)], lambda e: e.dma_start(
        out=Gt.rearrange("p (r n) -> p r n", r=8), in_=gout.ap().rearrange("(r p) n -> p r n", p=128)))
    Gv = Gt.rearrange("p (r l j n) -> p r l j n", r=8, l=2, j=12)
    so, _ = VLAY["bsel"]
    for l in range(DEPTH):
        m0 = MOD[:, l, :, 0].rearrange("p (r j) -> p r j", r=8)
        m1 = MOD[:, l, :, 1].rearrange("p (r j) -> p r j", r=8)
        ts(m0, Gv[:, :, l, :, 0], VEC[:, so:so + 1], None, ALU.mult, None, [("tn", 0), "VEC"], [("MOD", l)])
        for b_ in range(1, 4):
            stt(m0, Gv[:, :, l, :, b_], VEC[:, so + b_:so + b_ + 1], m0, ALU.mult, ALU.add, [("tn", 0), "VEC", ("MOD", l)], [("MOD", l)])
        T.op("dve", [("tn", 0)], [("MOD", l)], lambda e, m1=m1, l=l: e.tensor_copy(out=m1, in_=Gv[:, :, l, :, 4]))
    modkeys = [("MOD", l) for l in range(DEPTH)]
    lam_init = [0.8 - 0.6 * math.exp(-0.3 * l) for l in range(DEPTH)]
    for l in range(DEPTH):
        def modv(lo):
            return MOD[:, l, lo:lo + 16, :].rearrange("p a b -> p (a b)")
        stt(DER[:, l, 0, :], modv(16), 1.0, V("gpm2_%d" % l), ALU.add, ALU.mult, modkeys + ["VEC"], [("DER", l)])
        tt("dve", DER[:, l, 1, :], modv(32), V("gqm2_%d" % l), ALU.mult, modkeys + ["VEC"], [("DER", l)])
        stt(DER[:, l, 2, :], modv(64), 1.0, V("gpf2_%d" % l), ALU.add, ALU.mult, modkeys + ["VEC"], [("DER", l)])
        tt("dve", DER[:, l, 3, :], modv(80), V("gqf2_%d" % l), ALU.mult, modkeys + ["VEC"], [("DER", l)])
        tt("dve", LT[:], V("lq1_%d" % l), V("lk1_%d" % l), ALU.mult, ["VEC"], ["LT"])
        T.op("dve", ["LT"], [("LAM", l, 2)], lambda e, l=l: e.reduce_sum(out=LAM[:, l * 4 + 2:l * 4 + 3], in_=LT[:], axis=AX.X))
        tt("dve", LT[:], V("lq2_%d" % l), V("lk2_%d" % l), ALU.mult, ["VEC", ("LAM", l, 2)], ["LT"])
        T.op("dve", ["LT"], [("LAM", l, 3)], lambda e, l=l: e.reduce_sum(out=LAM[:, l * 4 + 3:l * 4 + 4], in_=LT[:], axis=AX.X))
        act(LAM[:, l * 4 + 2:l * 4 + 4], LAM[:, l * 4 + 2:l * 4 + 4], AF.Exp, [("LAM", l, 2), ("LAM", l, 3)], [("LAM", l, 2), ("LAM", l, 3)])
        tt("dve", LAM[:, l * 4:l * 4 + 1], LAM[:, l * 4 + 3:l * 4 + 4], LAM[:, l * 4 + 2:l * 4 + 3], ALU.subtract,
           [("LAM", l, 2), ("LAM", l, 3)], [("LAM", l, 0)])
        ts(LAM[:, l * 4:l * 4 + 1], LAM[:, l * 4:l * 4 + 1], -lam_init[l], None, ALU.add, None, [("LAM", l, 0)], [("LAM", l, 0)])
        ts(LAM[:, l * 4 + 1:l * 4 + 2], V("gsub_%d" % l), 1.0 - lam_init[l], None, ALU.mult, None, ["VEC"], [("LAM", l, 1)])
    dump("MOD", MOD[:].rearrange("p a b c -> p (a b c)"), [128, 384], F32, modkeys)
    dump("DER", DER[:].rearrange("p a b c -> p (a b c)"), [128, 256], F32, [("DER", 0), ("DER", 1)])
    dump("LAM", LAM[:], [128, 16], F32, [("LAM", l, i) for l in range(2) for i in range(4)])

    def scal(l, which, c, t):
        if which == "bA":
            return MOD[:, l, c, t:t + 1]
        if which == "bB":
            return MOD[:, l, 48 + c, t:t + 1]
        return DER[:, l, which, c * 2 + t:c * 2 + t + 1]

    def norm_to_HM(l, which_scale, which_bias, tcis):
        lo, hi = TCS[tcis[0]][0], TCS[tcis[-1]][0] + TCS[tcis[-1]][1]
        for c in range(KC):
            b = c % 2
            tt("dve", tn[:, b, lo:hi], YB[:, c, lo:hi], rs[:, lo:hi], ALU.mult,
               [ybk(c, ti) for ti in tcis] + [("rs", ti) for ti in tcis], [("tn", b)])
            for ti in tcis:
                s, w = TCS[ti]
                t = 1 if ti == 2 else 0
                act(HM[:, c, s:s + w], tn[:, b, s:s + w], AF.Identity, [("tn", b), ("DER", l)] + modkeys, [hmk(c, ti)],
                    bias=scal(l, which_bias, c, t), scale=scal(l, which_scale, c, t))

    fin_toks = []
    for l in range(DEPTH):
        last = (l == DEPTH - 1)
        alltc = [0, 1, 2]
        mixtc = [0, 1] if last else [0, 1, 2]
        norm_to_HM(l, 0, "bA", alltc)
        if l == 0:
            dump("h0", HM[:].rearrange("p a b -> p (a b)"), [128, KC * NT], BF16, [hmk(c, ti) for c in range(KC) for ti in range(3)])
        if stage <= 1:
            break
        T.barrier()
        T.dma("sp", [], ["rope"], lambda e: e.dma_start(out=rope[:], in_=rope_d[:, :, :]))
        T.dma("sp", [], ["dft32"], lambda e: e.dma_start(out=dft32[:], in_=dft_d[:, :]))
        T.dma("sp", [], ["wf32"], lambda e, l=l: e.dma_start(out=wf32[:], in_=w_f_d[l]))
        for zz in (zT, zcT):
            T.op("pool", [], ["zhalo"], lambda e, zz=zz: e.memset(zz[:], 0.0))
        T.op("dve", ["wf32"], ["wfh"], lambda e: e.tensor_copy(out=wfh[:], in_=wf32[:]))
        tt("dve", wfl[:], wf32[:], wfh[:], ALU.subtract, ["wf32", "wfh"], ["wfl"])
        for g in range(4):
            def fn(e, g=g):
                for (bk, o_) in ((0, 0), (1, 128)):
                    ob_ = bank(bk)[:, g * 128:(g + 1) * 128]
                    e.matmul(ob_, lhsT=dft32[:, o_:o_ + 128], rhs=wfh[:, g, :], start=True, stop=False)
                    e.matmul(ob_, lhsT=dft32[:, o_:o_ + 128], rhs=wfl[:, g, :], start=False, stop=False)
                    ins = e.matmul(ob_, lhsT=dft32[:, 256 + o_:256 + o_ + 128], rhs=wfh[:, g, :], start=False, stop=True)
                return ins
            T.op("pe", ["dft32", "wfh", "wfl"], [("ps", 0), ("ps", 1)], fn)
        for g in range(4):
            act(wAB[:, g, 0:128], bank(0)[:, g * 128:(g + 1) * 128], AF.Copy, [("ps", 0)], ["wAB"])
            act(wAB[:, g, 128:256], bank(1)[:, g * 128:(g + 1) * 128], AF.Copy, [("ps", 1)], ["wAB"])
        wiv = w_in_d[l].rearrange("(kc p) n -> p kc n", p=128)
        psrr = [0]

        def next_ps():
            b = 2 + psrr[0] % 4
            psrr[0] += 1
            return b

        def proj_group(slot, wkey, sub, ti):
            s, w = TCS[ti]
            b = next_ps()
            mm_group(bank(b)[:, 0:w], [(slot[:, kc, sub * 128:(sub + 1) * 128], HM[:, kc, s:s + w]) for kc in range(KC)],
                     [wkey] + [hmk(kc, ti) for kc in range(KC)], [("ps", b)])
            return b

        slot, wkey = load_w("w_in", l, 0, 16, 0, 512)
        for sub in range(4):
            for ti in mixtc:
                s, w = TCS[ti]
                b = proj_group(slot, wkey, sub, ti)
                act(finT[:, sub, s:s + w], bank(b)[:, 0:w], AF.Copy, [("ps", b)], [("finT", sub, ti)])
        ntk = 10 if not last else 8
        for tk in range(ntk):
            ti = 2 if tk >= 8 else tk // 4
            pbt = tk % 2
            pbi = 3 if pbt == 0 else 0

            def fn(e, tk=tk, pbi=pbi):
                ins = None
                for g in range(4):
                    ins = e.matmul(PB[pbi][:, g * 256:(g + 1) * 256], lhsT=finT[:, g, tk * 128:(tk + 1) * 128],
                                   rhs=wAB[:, g, :], start=True, stop=True)
                return ins
            T.op("pe", [("finT", g, ti) for g in range(4)] + ["wAB"], [("ps", pbi * 2), ("ps", pbi * 2 + 1)], fn)
            if tk < 8:
                ub = tk % 2
                T.op("dve", [("ps", pbi * 2), ("ps", pbi * 2 + 1)], [("uast", ub)],
                     lambda e, pbi=pbi, ub=ub: e.tensor_copy(out=uast[:, ub, :], in_=PB[pbi][:]))
                T.dma("sp", [("uast", ub)], ["fin"], lambda e, tk=tk, ub=ub: e.dma_start(
                    out=fin[tk * 128:(tk + 1) * 128, :], in_=uast[:, ub, :]))
            else:
                T.op("dve", [("ps", pbi * 2), ("ps", pbi * 2 + 1)], ["uabc"],
                     lambda e, pbi=pbi, tk=tk: e.tensor_copy(out=uabc[:, tk - 8, :], in_=PB[pbi][:]))
        T.custom("pool", ["fin"], ["fout"], lambda e: e.collective_compute(
            "AllGather", ALU.bypass, replica_groups=PAIRS, ins=[fin.ap().opt()], outs=[fout.ap().opt()]), ("cc", l, 0), 1)
        for hf in range(2):
            slot, wkey = load_w("w_in", l, 0, 16, 1024 + hf * 256, 1024 + hf * 256 + 256)
            for sub in range(2):
                for ti in mixtc:
                    s, w = TCS[ti]
                    b = proj_group(slot, wkey, sub, ti)
                    act(sg[:, sub, s:s + w], bank(b)[:, 0:w], AF.Sigmoid, [("ps", b)], [("sg", sub, ti)])
            slot, wkey = load_w("w_in", l, 0, 16, 512 + hf * 256, 512 + hf * 256 + 256)
            for sub in range(2):
                ch = hf * 2 + sub
                for ti in mixtc:
                    s, w = TCS[ti]
                    b = proj_group(slot, wkey, sub, ti)
                    if ti < 2:
                        dst = zT[:, ch, 15 + s:15 + s + w]
                    else:
                        dst = zcT[:, ch, 15:15 + NCX]
                    tt("dve", dst, bank(b)[:, 0:w], sg[:, sub, s:s + w], ALU.mult,
                       [("ps", b), ("sg", sub, ti), "zhalo"], [("z", ch, ti)])
        for ch in range(4):
            T.dma("sp", [("z", ch, 0)], ["zin"], lambda e, ch=ch: e.dma_start(
                out=zin[ch * 128:(ch + 1) * 128, 0:15], in_=zT[:, ch, 15:30]))
            T.dma("sp", [("z", ch, 1)], ["zin"], lambda e, ch=ch: e.dma_start(
                out=zin[ch * 128:(ch + 1) * 128, 15:30], in_=zT[:, ch, NL:NL + 15]))
        T.custom("pool", ["zin"], ["zout"], lambda e: e.collective_compute(
            "AllGather", ALU.bypass, replica_groups=PAIRS, ins=[zin.ap().opt()], outs=[zout.ap().opt()]), ("cc", l, 1), 1)
        pend_rope = []
        rope_n = [0]
        for pc in range(3, 7):
            slot, wkey = load_w("w_in", l, 0, 16, pc * 512, (pc + 1) * 512)
            isq = pc < 5
            for sub in range(4):
                hd = (pc - 3) % 2 * 4 + sub
                tcl = mixtc if isq else alltc
                for ti in tcl:
                    s, w = TCS[ti]
                    b = proj_group(slot, wkey, sub, ti)
                    if ti == 2:
                        dst = QT[:, hd, NL:NT] if isq else KcT[:, hd, :]
                        act(dst, bank(b)[:, 0:w], AF.Copy, [("ps", b)], [("QT", hd, 2) if isq else ("KcT", hd)])
                        continue
                    rb = rope_n[0] % 2
                    rope_n[0] += 1
                    act(t32[:, rb, :], bank(b)[:, 0:w], AF.Copy, [("ps", b)], [("t32", rb)])
                    T.op("pool", [("t32", rb)], [("tb", rb)], lambda e, rb=rb: e.tensor_copy(out=tb[:, rb, :], in_=t32[:, rb, :]))
                    for f in pend_rope:
                        f()
                    pend_rope[:] = []

                    def tail(rb=rb, s=s, w=w, hd=hd, ti=ti, isq=isq):
                        mm_group(bank(6 + rb)[:, :], [(perm16[:], tb[:, rb, :])], [("tb", rb), "perm16"], [("ps", 6 + rb)])
                        tt("dve", t32[:, rb, :], t32[:, rb, :], rope[:, 0, s:s + w], ALU.mult, [("t32", rb), "rope"], [("t32", rb)])
                        tt("dve", vv[:, rb, :], bank(6 + rb)[:, :], rope[:, 1, s:s + w], ALU.mult, [("ps", 6 + rb), "rope"], [("vv", rb)])
                        if isq:
                            tt("pool", QT[:, hd, s:s + w], t32[:, rb, :], vv[:, rb, :], ALU.add, [("t32", rb), ("vv", rb)], [("QT", hd, ti)])
                        else:
                            tt("pool", kst[:, rb, :], t32[:, rb, :], vv[:, rb, :], ALU.add, [("t32", rb), ("vv", rb)], [("kst", rb)])
                            T.dma("sp", [("kst", rb)], ["kin"], lambda e, hd=hd, s=s, w=w, rb=rb: e.dma_start(
                                out=kin[hd * 128:(hd + 1) * 128, s:s + w], in_=kst[:, rb, :]))
                    pend_rope.append(tail)
        for f in pend_rope:
            f()
        pend_rope[:] = []
        T.custom("pool", ["kin"], ["kout"], lambda e: e.collective_compute(
            "AllGather", ALU.bypass, replica_groups=PAIRS, ins=[kin.ap().opt()], outs=[kout.ap().opt()]), ("cc", l, 2), 1)
        for pc in range(7, 9):
            slot, wkey = load_w("w_in", l, 0, 16, pc * 512, (pc + 1) * 512)
            for tk in range(10):
                ti = 2 if tk >= 8 else tk // 4
                b = next_ps()
                mm_group(bank(b)[:, :], [(HM[:, kc, tk * 128:(tk + 1) * 128], slot[:, kc, :]) for kc in range(KC)],
                         [wkey] + [hmk(kc, ti) for kc in range(KC)], [("ps", b)])
                if tk < 8:
                    vb = tk % 2
                    act(vst[:, vb, :], bank(b)[:, :], AF.Copy, [("ps", b)], [("vst", vb)])
                    T.dma("sp", [("vst", vb)], ["vin"], lambda e, tk=tk, pc=pc, vb=vb: e.dma_start(
                        out=vin[tk * 128:(tk + 1) * 128, (pc - 7) * 512:(pc - 6) * 512], in_=vst[:, vb, :]))
                else:
                    act(Vc[:, tk - 8, (pc - 7) * 512:(pc - 6) * 512], bank(b)[:, :], AF.Copy, [("ps", b)], ["Vc"])
        T.custom("pool", ["vin"], ["vout"], lambda e: e.collective_compute(
            "AllGather", ALU.bypass, replica_groups=PAIRS, ins=[vin.ap().opt()], outs=[vout.ap().opt()]), ("cc", l, 3), 1)
        zov = zout.ap().rearrange("(r c p) n -> p r c n", r=2, c=4)
        T.dma("sp", ["zout"], ["hl"], lambda e: e.dma_start(out=hl[:, :, 0, :], in_=zov[:, 0, :, 15:30]))
        T.dma("sp", ["zout"], ["hl"], lambda e: e.dma_start(out=hl[:, :, 1, :], in_=zov[:, 1, :, 0:15]))
        mo, _ = VLAY["mask"]
        ts(zT[:, :, 0:15], hl[:, :, 0, :], VEC[:, mo:mo + 1], None, ALU.mult, None, ["hl", "VEC", "zhalo"], ["zhl"])
        ts(zT[:, :, NL + 15:NL + 30], hl[:, :, 1, :], VEC[:, mo + 1:mo + 2], None, ALU.mult, None, ["hl", "VEC", "zhalo"], ["zhr"])
        if l == 0:
            dump("QT", QT[:].rearrange("p a b -> p (a b)"), [128, 8 * NT], BF16, [("QT", h, t) for h in range(8) for t in range(3)])
            dump("zT", zT[:].rearrange("p a b -> p (a b)"), [128, 4 * (NL + 30)], BF16, [("z", c, t) for c in range(4) for t in range(2)] + ["zhl", "zhr"])
            dump("KcT", KcT[:].rearrange("p a b -> p (a b)"), [128, 8 * NCX], BF16, [("KcT", h) for h in range(8)])
            dump("Vc", Vc[:].rearrange("p a b -> p (a b)"), [128, 2048], BF16, ["Vc"])
            dump("uabc", uabc[:].rearrange("p a b -> p (a b)"), [128, 2048], BF16, ["uabc"])
            if dbg:
                for nm, tsr, shp in (("kout", kout, [2048, NL]), ("vout", vout, [2 * NL, 1024]), ("fout", fout, [2 * NL, 1024])):
                    tdb = nc.dram_tensor("dbg_" + nm, shp, BF16, kind="ExternalOutput")
                    dbg_outs.append(T.dma("sp", [nm], [("dbg", nm)], lambda e, tdb=tdb, tsr=tsr: e.dma_start(out=tdb.ap(), in_=tsr.ap())))
        if stage <= 2:
            break
        T.barrier()
        wo, _ = VLAY["wdw_%d" % l]
        for ch in range(4):
            db = ch % 2
            for k in range(31):
                ts(Dg[:, db, k, :], ident16[:], VEC[:, wo + ch * 31 + k:wo + ch * 31 + k + 1], None, ALU.mult, None,
                   ["ident16", "VEC"], [("Dg", db, k)])
            for ti in mixtc:
                s_, w = TCS[ti]
                b = 2 + (ch * 3 + ti) % 4
                if ti < 2:
                    pairs = [(Dg[:, db, k, :], zT[:, ch, s_ + k:s_ + k + w]) for k in range(31)]
                else:
                    pairs = [(Dg[:, db, k, :], zcT[:, ch, k:k + w]) for k in range(31)]
                mm_group(bank(b)[:, 0:w], pairs, [("Dg", db, k) for k in range(31)], [("ps", b)])
                act(czf[:, ch, s_:s_ + w], bank(b)[:, 0:w], AF.Identity, [("ps", b), "VEC"], [("czf", ch, ti)],
                    bias=V("bdw_%d" % l, ch, ch + 1), scale=1.0)
        lo, hi = 0, (NL if last else NT)
        for ch in range(4):
            kk = [("czf", ch, ti) for ti in mixtc]
            act(sq[:, 0, lo:hi], czf[:, ch, lo:hi], AF.Copy, kk, [("sq", 0)])
            act(sq[:, 1, lo:hi], czf[:, ch, lo:hi], AF.Square, kk, [("sq", 1)])

            def fn(e, ch=ch, mixtc=mixtc):
                ins = None
                for ti in mixtc:
                    s, w = TCS[ti]
                    e.matmul(bank(ti)[:, 0:w], lhsT=ones16[:], rhs=sq[:, 0, s:s + w], start=(ch == 0), stop=(ch == 3))
                    ins = e.matmul(bank(3 + ti)[:, 0:w], lhsT=ones16[:], rhs=sq[:, 1, s:s + w], start=(ch == 0), stop=(ch == 3))
                return ins
            T.op("pe", [("sq", 0), ("sq", 1), "ones16"], [("ps", i) for i in range(6)], fn)
        for ti in mixtc:
            s, w = TCS[ti]
            act(mean[:, s:s + w], bank(ti)[:, 0:w], AF.Copy, [("ps", ti)], [("tn", 0)], scale=1.0 / 512)
            act(ex2[:, s:s + w], bank(3 + ti)[:, 0:w], AF.Copy, [("ps", 3 + ti)], [("tn", 1)], scale=1.0 / 512)
            tt("dve", crs[:, s:s + w], mean[:, s:s + w], mean[:, s:s + w], ALU.mult, [("tn", 0)], [("rs", ti)])
            tt("dve", ex2[:, s:s + w], ex2[:, s:s + w], crs[:, s:s + w], ALU.subtract, [("tn", 1), ("rs", ti)], [("tn", 1)])
            act(crs[:, s:s + w], ex2[:, s:s + w], AF.Ln, [("tn", 1)], [("rs", ti)], bias=EPS, scale=1.0)
            act(crs[:, s:s + w], crs[:, s:s + w], AF.Exp, [("rs", ti)], [("rs", ti)], scale=-0.5)
        for ch in range(4):
            for ti in mixtc:
                s, w = TCS[ti]
                tt("dve", czf[:, ch, s:s + w], czf[:, ch, s:s + w], mean[:, s:s + w], ALU.subtract,
                   [("czf", ch, ti), ("tn", 0)], [("czf", ch, ti)])
                tt("dve", czf[:, ch, s:s + w], czf[:, ch, s:s + w], crs[:, s:s + w], ALU.mult,
                   [("czf", ch, ti), ("rs", ti)], [("czf", ch, ti)])
                act(sz[:, ch, s:s + w], czf[:, ch, s:s + w], AF.Silu, [("czf", ch, ti), "VEC"], [("sz", ch, ti)],
                    bias=V("bln_%d" % l, ch, ch + 1), scale=V("gln_%d" % l, ch, ch + 1))
        T.dma("pool", [], ["wpw"], lambda e, l=l: e.dma_start(out=wpw[:], in_=w_pw_d[l].rearrange("(kc p) n -> p kc n", p=128)))
        for ec in range(4):
            for ti in mixtc:
                s, w = TCS[ti]
                b = 6 + (ec * 3 + ti) % 2
                mm_group(bank(b)[:, 0:w], [(wpw[:, kc, ec * 128:(ec + 1) * 128], sz[:, kc, s:s + w]) for kc in range(4)],
                         ["wpw"] + [("sz", kc, ti) for kc in range(4)], [("ps", b)])
                act(HM[:, 4 + ec, s:s + w], bank(b)[:, 0:w], AF.Identity, [("ps", b), "VEC"], [hmk(4 + ec, ti)],
                    bias=V("bpw_%d" % l, ec, ec + 1), scale=1.0)
        T.barrier()
        for j in range(16):
            ub = j % 3
            T.dma("sp", ["fout"], [("uabj", ub)], lambda e, j=j, ub=ub: e.dma_start(out=uabj[:, ub, :], in_=fout[j * 128:(j + 1) * 128, :]))
            T.dma("sp", [], [("tab", ub)], lambda e, j=j, ub=ub: e.dma_start(out=tab[:, ub, 0, :], in_=CL_d[j * 128:(j + 1) * 128, :]))
            T.dma("sp", [], [("tab", ub)], lambda e, j=j, ub=ub: e.dma_start(out=tab[:, ub, 1, :], in_=SL_d[j * 128:(j + 1) * 128, :]))

            def fn(e, j=j, ub=ub):
                ins = None
                for g in range(4):
                    for n in range(2):
                        e.matmul(bank(g * 2 + n)[:, :], lhsT=uabj[:, ub, g * 256:g * 256 + 128], rhs=tab[:, ub, 0, n * 512:(n + 1) * 512],
                                 start=(j == 0), stop=False)
                    for n in range(2):
                        ins = e.matmul(bank(g * 2 + n)[:, :], lhsT=uabj[:, ub, g * 256 + 128:g * 256 + 256],
                                       rhs=tab[:, ub, 1, n * 512:(n + 1) * 512], start=False, stop=(j == 15))
                return ins
            T.op("pe", [("uabj", ub), ("tab", ub)], [("ps", b_) for b_ in range(8)], fn)
        for g in range(4):
            for n in range(2):
                act(HM[:, g, n * 512:(n + 1) * 512], bank(g * 2 + n)[:, :], AF.Copy, [("ps", g * 2 + n)], [hmk(g, n)])
        if not last:
            for j in range(2):
                T.dma("sp", [], ["tabc"], lambda e, j=j: e.dma_start(out=tabc[:, j, 0, :], in_=Cc_d[j * 128:(j + 1) * 128, :]))
                T.dma("sp", [], ["tabc"], lambda e, j=j: e.dma_start(out=tabc[:, j, 1, :], in_=Sc_d[j * 128:(j + 1) * 128, :]))
            for g in range(4):
                pairs = []
                for j in range(2):
                    pairs.append((uabc[:, j, g * 256:g * 256 + 128], tabc[:, j, 0, :]))
                    pairs.append((uabc[:, j, g * 256 + 128:g * 256 + 256], tabc[:, j, 1, :]))
                b = 4 + g % 2
                mm_group(bank(b)[:, 0:NCX], pairs, ["uabc", "tabc"], [("ps", b)])
                act(HM[:, g, NL:NT], bank(b)[:, 0:NCX], AF.Copy, [("ps", b)], [hmk(g, 2)])
        kov = kout.ap().rearrange("(r h p) t -> p r h t", r=2, h=8)
        neglam = LAM[:, l * 4:l * 4 + 1]
        gsubs = LAM[:, l * 4 + 1:l * 4 + 2]
        it_n, s_n, p_n, q_n = [0], [0], [0], [0]
        pending = []
        pend_sum = []
        pend_pv = []

        def run_pending(upto):
            keep = []
            for st_, f in pending:
                if st_ <= upto:
                    f()
                else:
                    keep.append([st_, f])
            pending[:] = keep

        for hd in range(8):
            hb = hd % 2
            T.dma("sp", ["kout"], [("Kh", hb)], lambda e, hd=hd, hb=hb: e.dma_start(
                out=Kh[:, hb, :].rearrange("p (r t) -> p r t", r=2), in_=kov[:, :, hd, :]))
            T.dma("sp", ["vout"], [("Vh", hb)], lambda e, hd=hd, hb=hb: e.dma_start(
                out=Vh[:, hb, :, :], in_=vout.ap().rearrange("(kc p) e -> p kc e", p=128)[:, :, hd * 128:(hd + 1) * 128]))
            qlist = [(qs, list(range(18))) for qs in (0, 256, 512, 768)]
            if not last:
                qlist.append((NL, [0, 1]))
            for (qs, kcs) in qlist:
                qti = 2 if qs >= NL else qs // 512
                it = it_n[0]
                it_n[0] += 1
                par = it % 2
                ob, sbk = 4 + par, 6 + par
                nbk = sbk

                def k_ap(kc, m):
                    if kc < 2:
                        return KcT[m * 64:(m + 1) * 64, hd, kc * 128:(kc + 1) * 128]
                    return Kh[m * 64:(m + 1) * 64, hb, (kc - 2) * 128:(kc - 1) * 128]

                def v_ap(kc):
                    if kc < 2:
                        return Vc[:, kc, hd * 128:(hd + 1) * 128]
                    return Vh[:, hb, kc - 2, :]

                def kkeys(kc):
                    return [("KcT", hd)] if kc < 2 else [("Kh", hb)]

                def vkeys(kc):
                    return ["Vc"] if kc < 2 else [("Vh", hb)]

                def s_op(kc):
                    sb_ = s_n[0] % 2
                    s_n[0] += 1
                    aps = (PB[sb_][:, 0:256], k_ap(kc, 0), QT[0:64, hd, qs:qs + 256],
                           PB[sb_][:, 512:768], k_ap(kc, 1), QT[64:128, hd, qs:qs + 256])

                    def fn(e, aps=aps):
                        e.matmul(aps[0], lhsT=aps[1], rhs=aps[2], start=True, stop=True)
                        return e.matmul(aps[3], lhsT=aps[4], rhs=aps[5], start=True, stop=True)
                    T.op("pe", kkeys(kc) + [("QT", hd, qti)], [("ps", sb_ * 2), ("ps", sb_ * 2 + 1)], fn)
                    return sb_

                pend_s = s_op(kcs[0])
                for i, kc in enumerate(kcs):
                    sb_ = pend_s
                    pb_ = p_n[0] % 4
                    p_n[0] += 1
                    act(pt[:, pb_, :].rearrange("p (m q) -> p m q", m=2),
                        PB[sb_][:].rearrange("p (m q) -> p m q", m=2)[:, :, 0:256], AF.Exp,
                        [("ps", sb_ * 2), ("ps", sb_ * 2 + 1)], [("pt", pb_)], scale=0.125)
                    if i + 1 < len(kcs):
                        pend_s = s_op(kcs[i + 1])
                    aps = (v_ap(kc), pt[:, pb_, :], bank(ob)[:, :])

                    def fn(e, aps=aps, first=(i == 0), lastk=(i == len(kcs) - 1)):
                        return e.matmul(aps[2], lhsT=aps[0], rhs=aps[1], start=first, stop=lastk)
                    if pend_pv:
                        pend_pv.pop()()
                    pend_pv.append(lambda fn=fn, rk=vkeys(kc) + [("pt", pb_)], ob=ob: T.op("pe", rk, [("ps", ob)], fn))
                    if pend_sum:
                        pend_sum.pop()()
                    if i % 2 == 0:
                        pb_even = pb_
                    else:
                        qb_ = q_n[0] % 2
                        q_n[0] += 1
                        tt("dve", pts[:, qb_, :], pt[:, pb_even, :], pt[:, pb_, :], ALU.add,
                           [("pt", pb_even), ("pt", pb_)], [("pts", qb_)])
                        pend_sum.append(lambda qb_=qb_, sbk=sbk, first=(i == 1), lastp=(i == len(kcs) - 1): T.op(
                            "pe", [("pts", qb_), "ones16"], [("ps", sbk)], lambda e: e.matmul(
                                bank(sbk)[:, :], lhsT=ones16[:], rhs=pts[:, qb_, :], start=first, stop=lastp)))
                    run_pending(i)
                if pend_pv:
                    pend_pv.pop()()
                if pend_sum:
                    pend_sum.pop()()
                run_pending(10 ** 9)

                def mk(hd=hd, qs=qs, qti=qti, par=par, ob=ob, sbk=sbk, nbk=nbk):
                    def A1():
                        act(frA[:, par, :], bank(sbk)[:, :], AF.Ln, [("ps", sbk)], [("frA", par)])
                        act(frA[:, par, :], frA[:, par, :], AF.Exp, [("frA", par)], [("frA", par)], scale=-1.0)
                        tt("dve", fo2[:, par, :], bank(ob)[:, :], frA[:, par, :], ALU.mult, [("ps", ob), ("frA", par)], [("fo2", par)])
                        stt(fot[:, par, :], fo2[:, par, 256:512], neglam, fo2[:, par, 0:256], ALU.mult, ALU.add,
                            [("fo2", par), ("LAM", l, 0)], [("fot", par)])

                    def A2():
                        act(sqo[:, par, :], fot[:, par, :], AF.Square, [("fot", par)], [("sqo", par)])

                    def PEm():
                        mm_group(bank(nbk)[:, 0:256], [(ones16[:], sqo[:, par, :])], [("sqo", par), "ones16"], [("ps", nbk)])

                    def B1():
                        act(frB[:, par, :], bank(nbk)[:, 0:256], AF.Ln, [("ps", nbk)], [("frB", par)], bias=EPS, scale=1.0 / 128)
                        act(frB[:, par, :], frB[:, par, :], AF.Exp, [("frB", par)], [("frB", par)], scale=-0.5)

                    def B2():
                        tt("dve", fot[:, par, :], fot[:, par, :], frB[:, par, :], ALU.mult, [("fot", par), ("frB", par)], [("fot", par)])

                    def B3():
                        act(HM[:, 8 + hd, qs:qs + 256], fot[:, par, :], AF.Identity, [("fot", par), ("LAM", l, 1)],
                            [hmk(8 + hd, qti)], bias=0.0, scale=gsubs)
                    return A1, A2, PEm, B1, B2, B3
                A1, A2, PEm, B1, B2, B3 = mk()
                A1()
                pending.extend([[2, A2], [4, PEm], [5, B1], [7, B2], [8, B3]])
        run_pending(10 ** 9)
        if l == 0:
            dump("mix0", HM[:].rearrange("p a b -> p (a b)"), [128, KC * NT], BF16, [hmk(c, ti) for c in range(KC) for ti in range(3)])
        if stage <= 3:
            break
        T.barrier()
        rr = [0]
        pend_stat = []
        for pc in range(4):
            slot, wkey = load_w("w_out", l, 0, 16, pc * 512, (pc + 1) * 512)
            for sub in range(4):
                dc = pc * 4 + sub
                for ti in mixtc:
                    s, w = TCS[ti]
                    b = 3 + rr[0] % 3
                    qb = rr[0] % 2
                    rr[0] += 1
                    mm_group(bank(b)[:, 0:w], [(slot[:, kc, sub * 128:(sub + 1) * 128], HM[:, kc, s:s + w]) for kc in range(KC)],
                             [wkey] + [hmk(kc, ti) for kc in range(KC)], [("ps", b)])
                    for f in pend_stat:
                        f()
                    pend_stat = []
                    act(YB[:, dc, s:s + w], bank(b)[:, 0:w], AF.Copy, [("ps", b)], [ybk(dc, ti)])
                    act(sq[:, qb, 0:w], bank(b)[:, 0:w], AF.Square, [("ps", b)], [("sq", qb)])
                    pend_stat.append(lambda ti=ti, w=w, qb=qb, dc=dc: T.op(
                        "pe", [("sq", qb), "ones16"], [("ps", ti)], lambda e: e.matmul(
                            bank(ti)[:, 0:w], lhsT=ones16[:], rhs=sq[:, qb, 0:w], start=(dc == 0), stop=(dc == KC - 1))))
        for f in pend_stat:
            f()
        for ti in mixtc:
            s, w = TCS[ti]
            act(rs[:, s:s + w], bank(ti)[:, 0:w], AF.Ln, [("ps", ti)], [("rs", ti)], bias=EPS, scale=1.0 / D)
            act(rs[:, s:s + w], rs[:, s:s + w], AF.Exp, [("rs", ti)], [("rs", ti)], scale=-0.5)
        lo, hi = 0, (NL if last else NT)

        def resid(l, gate_which, final, stat_tcis=None):
            def load_x(dc):
                xb = dc % 2
                if l == 0 and not final:
                    T.dma("sp", [], [("tn", xb)], lambda e, dc=dc, xb=xb: e.dma_start(out=tn[:, xb, 0:NL], in_=xT_d[dc * 128:(dc + 1) * 128, :]))
                    T.dma("sp", [], [("tn", xb)], lambda e, dc=dc, xb=xb: e.dma_start(out=tn[:, xb, NL:NT], in_=cxT_d[dc * 128:(dc + 1) * 128, :]))
                else:
                    T.dma("sp", [("xs", dc)], [("tn", xb)], lambda e, dc=dc, xb=xb, lo=lo, hi=hi: e.dma_start(out=tn[:, xb, lo:hi], in_=xs_d.ap()[dc, :, lo:hi]))
            load_x(0)
            load_x(1)
            for dc in range(KC):
                xb = dc % 2
                kk = [ybk(dc, ti) for ti in mixtc]
                tt("dve", YB[:, dc, lo:hi], YB[:, dc, lo:hi], rs[:, lo:hi], ALU.mult, kk + [("rs", ti) for ti in mixtc], kk)
                stt(YB[:, dc, 0:NL], YB[:, dc, 0:NL], scal(l, gate_which, dc, 0), tn[:, xb, 0:NL], ALU.mult, ALU.add,
                    [ybk(dc, 0), ybk(dc, 1), ("tn", xb), ("DER", l)], [ybk(dc, 0), ybk(dc, 1)])
                if not last:
                    stt(YB[:, dc, NL:NT], YB[:, dc, NL:NT], scal(l, gate_which, dc, 1), tn[:, xb, NL:NT], ALU.mult, ALU.add,
                        [ybk(dc, 2), ("tn", xb), ("DER", l)], [ybk(dc, 2)])
                if final and last:
                    fin_toks.append(T.dma("sp", [ybk(dc, 0), ybk(dc, 1)], [("out", dc)], lambda e, dc=dc: e.dma_start(
                        out=out_d[dc * 128:(dc + 1) * 128, :], in_=YB[:, dc, 0:NL])))
                else:
                    T.dma("sp", kk, [("xs", dc)], lambda e, dc=dc, lo=lo, hi=hi: e.dma_start(out=xs_d.ap()[dc, :, lo:hi], in_=YB[:, dc, lo:hi]))
                if dc + 2 < KC:
                    load_x(dc + 2)
                if stat_tcis is not None:
                    stats_chunk(dc, YB[:, dc, :], [ybk(dc, ti) for ti in stat_tcis], stat_tcis, KC, 0)
            if stat_tcis is not None:
                rstd_finish(stat_tcis, rs, "rs", D, 0)

        resid(l, 1, False, stat_tcis=mixtc)
        norm_to_HM(l, 2, "bB", mixtc)
        if l == 0:
            dump("x1", YB[:].rearrange("p a b -> p (a b)"), [128, KC * NT], F32, [ybk(c, ti) for c in range(KC) for ti in range(3)])
        if stage <= 4:
            break
        w1v = w_m1_d[l].rearrange("(kc p) n -> p kc n", p=128)
        w2v = w_m2_d[l].rearrange("(kc p) n -> p kc n", p=128)
        rr = [0]
        for fb in range(8):
            for pi in range(2):
                slot, wkey = load_w("w_m1", l, 0, 16, fb * 1024 + pi * 512, fb * 1024 + pi * 512 + 512)
                for sub in range(4):
                    hc = pi * 4 + sub
                    for ti in mixtc:
                        s, w = TCS[ti]
                        b = rr[0] % 4
                        rb = rr[0] % 2
                        rr[0] += 1
                        mm_group(bank(b)[:, 0:w], [(slot[:, kc, sub * 128:(sub + 1) * 128], HM[:, kc, s:s + w]) for kc in range(KC)],
                                 [wkey] + [hmk(kc, ti) for kc in range(KC)], [("ps", b)])
                        ts(rl[:, rb, 0:w], bank(b)[:, 0:w], 0.0, None, ALU.max, None, [("ps", b)], [("rl", rb)])
                        act(hid[:, hc, s:s + w], rl[:, rb, 0:w], AF.Square, [("rl", rb)], [("hid", hc, ti)])
            for pj in range(4):
                slot, wkey = load_w("w_m2", l, fb * 8, (fb + 1) * 8, pj * 512, (pj + 1) * 512)
                for sub in range(4):
                    dc = pj * 4 + sub
                    for ti in mixtc:
                        s, w = TCS[ti]
                        b = 4 + rr[0] % 4
                        rr[0] += 1
                        mm_group(bank(b)[:, 0:w], [(slot[:, kc, sub * 128:(sub + 1) * 128], hid[:, kc, s:s + w]) for kc in range(8)],
                                 [wkey] + [("hid", kc, ti) for kc in range(8)], [("ps", b)])
                        if fb == 0:
                            act(YB[:, dc, s:s + w], bank(b)[:, 0:w], AF.Copy, [("ps", b)], [ybk(dc, ti)])
                        else:
                            tt("dve", YB[:, dc, s:s + w], YB[:, dc, s:s + w], bank(b)[:, 0:w], ALU.add, [("ps", b), ybk(dc, ti)], [ybk(dc, ti)])
        sumsq_rstd(lambda c: YB[:, c, :], lambda c: [ybk(c, ti) for ti in mixtc], mixtc, KC, rs, "rs", D, 0)
        resid(l, 3, True, stat_tcis=(None if last else alltc))
        if l == 0:
            dump("x2", YB[:].rearrange("p a b -> p (a b)"), [128, KC * NT], F32, [ybk(c, ti) for c in range(KC) for ti in range(3)])
        if stage <= 5:
            break

    if not fin_toks:
        fin_toks.append(T.dma("sp", [], [("out", 0)], lambda e: e.dma_start(out=out_d[0:128, :], in_=tn[:, 0, 0:NL])))
    if wseq is None:
        return wrec
    T.final_wait("sp", fin_toks + dbg_outs)
    with contextlib.ExitStack() as st:
        T.emit(nc, st)
    return nc


def host_inputs(inp):
    f32 = np.float32
    x = np.asarray(inp["x"], f32)
    ctx = np.asarray(inp["ctx"], f32)
    c = np.asarray(inp["c"], f32)
    c_ctx = np.asarray(inp["c_ctx"], f32)
    shared = {
        "w_in": np.ascontiguousarray(inp["w_in"], dtype=f32),
        "w_out": np.ascontiguousarray(inp["w_out"], dtype=f32),
        "w_f": np.ascontiguousarray(np.transpose(np.asarray(inp["w_fourier"], f32), (0, 2, 1, 3))),
        "w_pw": np.ascontiguousarray(inp["w_conv_pw"], dtype=f32),
        "w_m1": np.ascontiguousarray(inp["w_mlp_in"], dtype=f32),
        "w_m2": np.ascontiguousarray(inp["w_mlp_out"], dtype=f32),
    }
    k = np.arange(128)
    ang = 2 * np.pi * np.outer(k, k) / 128.0
    cs_ = np.concatenate([np.cos(ang), np.sin(ang)], axis=1).astype(f32)
    cs_hi = cs_.astype(ml_dtypes.bfloat16)
    cs_lo = (cs_ - cs_hi.astype(f32)).astype(ml_dtypes.bfloat16)
    shared["dftc"] = np.ascontiguousarray(np.concatenate([cs_hi, cs_lo], axis=1))
    kk = np.arange(NCX)
    angc = 2 * np.pi * (np.outer(kk, kk) % NCX) / NCX
    nrm_c = 1.0 / math.sqrt(NCX * 128)
    shared["Cc"] = (np.cos(angc) * nrm_c).astype(f32).astype(ml_dtypes.bfloat16)
    shared["Scn"] = (-np.sin(angc) * nrm_c).astype(f32).astype(ml_dtypes.bfloat16)
    perm = np.zeros((128, 128), f32)
    for p in range(128):
        j = p % 64
        q = p + 32 if j < 32 else p - 32
        perm[q, p] = 1.0
    shared["perm"] = perm
    shared["ident"] = np.eye(128, dtype=f32)
    n_freq = 16
    inv = (10000.0 ** (-np.arange(n_freq, dtype=np.float64) / n_freq))
    L = 2048
    nrm = 1.0 / math.sqrt(L * 128)
    half_tabs = []
    for half in range(2):
        idx = np.arange(half * NL, (half + 1) * NL)
        row = (idx // 64).astype(np.float64)
        col = (idx % 64).astype(np.float64)
        a = np.concatenate([row[:, None] * inv, col[:, None] * inv], axis=-1)
        a = a.astype(np.float32).astype(np.float64)
        cs, sn = np.cos(a), np.sin(a)
        cosT = np.zeros((128, NL))
        sinT = np.zeros((128, NL))
        for p in range(128):
            j = p % 64
            cosT[p] = cs[:, j % 32]
            sinT[p] = -sn[:, j] if j < 32 else sn[:, j - 32]
        rope = np.stack([cosT, sinT], axis=1).astype(f32)
        jj = np.arange(L)
        angL = 2 * np.pi * ((np.outer(jj, idx)) % L) / L
        CL = (np.cos(angL) * nrm).astype(f32).astype(ml_dtypes.bfloat16)
        SLn = (-np.sin(angL) * nrm).astype(f32).astype(ml_dtypes.bfloat16)
        half_tabs.append((rope, CL, SLn))
    maps = []
    for r in range(8):
        b, half = r // 2, r % 2
        vec = np.zeros((128, NV), f32)

        def put(name, arr):
            o, w = VLAY[name]
            assert arr.shape == (128, w), (name, arr.shape, w)
            vec[:, o:o + w] = arr
        for l in range(DEPTH):
            put("badap_%d" % l, np.ascontiguousarray(np.repeat(chunked(inp["b_ada"][l])[:, r * 12:(r + 1) * 12], 5, axis=1)))
            put("gpm2_%d" % l, dup2(chunked(inp["g_pre_mix"][l])))
            put("gqm2_%d" % l, dup2(chunked(inp["g_post_mix"][l])))
            put("gpf2_%d" % l, dup2(chunked(inp["g_pre_mlp"][l])))
            put("gqf2_%d" % l, dup2(chunked(inp["g_post_mlp"][l])))
            wdw = np.asarray(inp["w_dw"][l], f32)
            put("wdw_%d" % l, np.ascontiguousarray(wdw.T.reshape(4, 128, 31).transpose(1, 0, 2).reshape(128, 124)))
            put("bdw_%d" % l, chunked(inp["b_dw"][l]))
            put("gln_%d" % l, chunked(inp["g_conv_ln"][l]))
            put("bln_%d" % l, chunked(inp["b_conv_ln"][l]))
            put("bpw_%d" % l, chunked(inp["b_conv_pw"][l]))
            put("gsub_%d" % l, chunked(inp["g_subln"][l]))
            for nm, key in (("lq1", "lambda_q1"), ("lk1", "lambda_k1"), ("lq2", "lambda_q2"), ("lk2", "lambda_k2")):
                put("%s_%d" % (nm, l), np.broadcast_to(np.asarray(inp[key][l], f32)[None, :], (128, 64)))
        cv = np.stack([chunked(c[i]) for i in range(4)] + [chunked(c_ctx)], axis=2).reshape(128, 80)
        put("cvec5", cv)
        sel = np.zeros((128, 4), f32)
        sel[:, b] = 1.0
        put("bsel", sel)
        put("mask", np.broadcast_to(np.array([[1.0 if half == 1 else 0.0, 1.0 if half == 0 else 0.0]], f32), (128, 2)))
        rope, CL, SLn = half_tabs[half]
        m = dict(shared)
        m["w_ada"] = np.ascontiguousarray(np.asarray(inp["w_ada"], f32)[:, :, r * 1536:(r + 1) * 1536])
        m["xT"] = np.ascontiguousarray(x[b, half * NL:(half + 1) * NL, :].T)
        m["cxT"] = np.ascontiguousarray(ctx[b].T)
        m["vec"] = vec
        m["rope"] = rope
        m["CL"] = CL
        m["SLn"] = SLn
        maps.append(m)
    return maps


_NC_CACHE = {}


def kernel(**inputs):
    stage = int(os.environ.get("KSTAGE", "99"))
    dbg = os.environ.get("KDBG", "0") == "1"
    key = (stage, dbg)
    if key not in _NC_CACHE:
        rec = build(stage, dbg, None)
        _NC_CACHE[key] = build(stage, dbg, rec)
    nc = _NC_CACHE[key]
    maps = host_inputs(inputs)
    res = run_bass_kernel_spmd(nc, maps, core_ids=list(range(8)))
    out = np.empty((4, 2048, D), np.float32)
    for r in range(8):
        b, half = r // 2, r % 2
        out[b, half * NL:(half + 1) * NL, :] = res.results[r]["outT"].T
    if dbg:
        kernel.last_results = res.results
    return out
```
